# Optimizing a Trainium2 kernel written in Bass

```python
import math
import jax, jax.numpy as jnp
from jax import lax
import numpy as np

D_MODEL = 1024
BATCH = 2
SEQ = 8192
DEPTH = 1

HEAD_DIM = 64
ATTN_WIDTH = D_MODEL // 2
CONV_WIDTH = D_MODEL - ATTN_WIDTH
N_ATTN_HEADS = ATTN_WIDTH // HEAD_DIM
CONV_GROUPS = CONV_WIDTH // HEAD_DIM
CONV_KERNEL = 31
D_FF = 4 * D_MODEL
Q_BLOCK = 128
LN_EPS = 1e-5
DEEPNORM_ALPHA = (2.0 * DEPTH) ** 0.25
DEEPNORM_BETA = (8.0 * DEPTH) ** -0.25
IN_SPLITS = [ATTN_WIDTH, 2 * ATTN_WIDTH, 3 * ATTN_WIDTH, 3 * ATTN_WIDTH + N_ATTN_HEADS,
             3 * ATTN_WIDTH + N_ATTN_HEADS + CONV_WIDTH]
N_IN_COLS = 3 * ATTN_WIDTH + N_ATTN_HEADS + 2 * CONV_WIDTH

kernel_name = "hymba_fox_conformer_deepnorm_adaln_block"


def _layernorm(x, g, b):
    xf = x.astype(jnp.float32)
    mu = jnp.mean(xf, axis=-1, keepdims=True)
    var = jnp.mean(jnp.square(xf - mu), axis=-1, keepdims=True)
    return ((xf - mu) * lax.rsqrt(var + LN_EPS)).astype(x.dtype) * g + b


def _rmsnorm(x, g):
    xf = x.astype(jnp.float32)
    return (xf * lax.rsqrt(jnp.mean(xf * xf, axis=-1, keepdims=True) + LN_EPS)).astype(x.dtype) * g


def _forgetting_attention(q, k, v, log_f):
    b, h, s, dh = q.shape
    n_blk = s // Q_BLOCK
    cum = jnp.cumsum(log_f, axis=-1)
    q_blocks = (q * (dh ** -0.5)).reshape(b, h, n_blk, Q_BLOCK, dh).transpose(2, 0, 1, 3, 4)
    cq_blocks = cum.reshape(b, h, n_blk, Q_BLOCK).transpose(2, 0, 1, 3)
    k_pos = jnp.arange(s)

    def one_block(args):
        qb, cqb, blk = args
        q_pos = blk * Q_BLOCK + jnp.arange(Q_BLOCK)
        logits = jnp.einsum('bhqd,bhkd->bhqk', qb, k).astype(jnp.float32)
        logits = logits + cqb[..., :, None] - cum[..., None, :]
        causal = k_pos[None, :] <= q_pos[:, None]
        logits = jnp.where(causal, logits, -jnp.inf)
        p = jax.nn.softmax(logits, axis=-1)
        return jnp.einsum('bhqk,bhkd->bhqd', p.astype(v.dtype), v)

    out = lax.map(one_block, (q_blocks, cq_blocks, jnp.arange(n_blk)))
    return out.transpose(1, 2, 0, 3, 4).reshape(b, h, s, dh)


def _conformer_conv(a, gate, w_dw, b_dw, gn_g, gn_b):
    u = a * jax.nn.sigmoid(gate)
    y = lax.conv_general_dilated(u, w_dw, window_strides=(1,), padding=((CONV_KERNEL - 1, 0),),
                                 dimension_numbers=('NWC', 'WIO', 'NWC'),
                                 feature_group_count=CONV_WIDTH) + b_dw
    bsz, seq, ch = y.shape
    yg = y.reshape(bsz, seq, CONV_GROUPS, ch // CONV_GROUPS).astype(jnp.float32)
    mu = jnp.mean(yg, axis=-1, keepdims=True)
    var = jnp.mean(jnp.square(yg - mu), axis=-1, keepdims=True)
    yn = ((yg - mu) * lax.rsqrt(var + LN_EPS)).reshape(bsz, seq, ch).astype(y.dtype) * gn_g + gn_b
    return jax.nn.silu(yn)


def setup_inputs(seed: int = 0) -> dict:
    key = jax.random.key(seed)
    ks = jax.random.split(key, 24)
    nrm = jax.random.normal
    d = D_MODEL
    x = nrm(ks[0], (BATCH, SEQ, d), jnp.float32)
    c = nrm(ks[1], (BATCH, d), jnp.float32)
    w_ada = nrm(ks[2], (DEPTH, d, 6 * d), jnp.float32) * (0.1 * d ** -0.5)
    b_ada = nrm(ks[3], (DEPTH, 6 * d), jnp.float32) * 0.02
    col_scale = jnp.concatenate([
        jnp.ones((2 * ATTN_WIDTH,), jnp.float32),
        jnp.full((ATTN_WIDTH,), DEEPNORM_BETA, jnp.float32),
        jnp.ones((N_ATTN_HEADS,), jnp.float32),
        jnp.full((CONV_WIDTH,), DEEPNORM_BETA, jnp.float32),
        jnp.ones((CONV_WIDTH,), jnp.float32)])
    w_in = nrm(ks[4], (DEPTH, d, N_IN_COLS), jnp.float32) * (d ** -0.5) * col_scale
    b_forget = jax.random.uniform(ks[5], (DEPTH, N_ATTN_HEADS), jnp.float32, minval=2.0, maxval=6.0)
    w_dw = nrm(ks[6], (DEPTH, CONV_KERNEL, 1, CONV_WIDTH), jnp.float32) * (CONV_KERNEL ** -0.5)
    b_dw = nrm(ks[7], (DEPTH, CONV_WIDTH), jnp.float32) * 0.02
    gn_g = 1.0 + 0.02 * nrm(ks[8], (DEPTH, CONV_WIDTH), jnp.float32)
    gn_b = 0.02 * nrm(ks[9], (DEPTH, CONV_WIDTH), jnp.float32)
    g_attn_out = 1.0 + 0.02 * nrm(ks[10], (DEPTH, ATTN_WIDTH), jnp.float32)
    g_conv_out = 1.0 + 0.02 * nrm(ks[11], (DEPTH, CONV_WIDTH), jnp.float32)
    w_out = nrm(ks[12], (DEPTH, d, d), jnp.float32) * (d ** -0.5) * DEEPNORM_BETA
    ln1_g = 1.0 + 0.02 * nrm(ks[13], (DEPTH, d), jnp.float32)
    ln1_b = 0.02 * nrm(ks[14], (DEPTH, d), jnp.float32)
    w_ff1 = nrm(ks[15], (DEPTH, d, D_FF), jnp.float32) * (d ** -0.5) * DEEPNORM_BETA
    w_ff2 = nrm(ks[16], (DEPTH, D_FF, d), jnp.float32) * (D_FF ** -0.5) * DEEPNORM_BETA
    ln2_g = 1.0 + 0.02 * nrm(ks[17], (DEPTH, d), jnp.float32)
    ln2_b = 0.02 * nrm(ks[18], (DEPTH, d), jnp.float32)
    return {"x": x, "c": c, "w_ada": w_ada, "b_ada": b_ada, "w_in": w_in, "b_forget": b_forget,
            "w_dw": w_dw, "b_dw": b_dw, "gn_g": gn_g, "gn_b": gn_b, "g_attn_out": g_attn_out,
            "g_conv_out": g_conv_out, "w_out": w_out, "ln1_g": ln1_g, "ln1_b": ln1_b,
            "w_ff1": w_ff1, "w_ff2": w_ff2, "ln2_g": ln2_g, "ln2_b": ln2_b}


def reference(x, c, w_ada, b_ada, w_in, b_forget, w_dw, b_dw, gn_g, gn_b, g_attn_out,
              g_conv_out, w_out, ln1_g, ln1_b, w_ff1, w_ff2, ln2_g, ln2_b):
    bsz, seq, _ = x.shape
    for layer in range(DEPTH):
        ada = jax.nn.silu(c) @ w_ada[layer] + b_ada[layer]
        sh1, sc1, gt1, sh2, sc2, gt2 = jnp.split(ada[:, None, :], 6, axis=-1)

        u = x * (1 + sc1) + sh1
        proj = u @ w_in[layer]
        q, k, v, f_logit, a, g = jnp.split(proj, IN_SPLITS, axis=-1)

        def heads(t):
            return t.reshape(bsz, seq, N_ATTN_HEADS, HEAD_DIM).transpose(0, 2, 1, 3)

        log_f = jax.nn.log_sigmoid((f_logit + b_forget[layer]).astype(jnp.float32)).transpose(0, 2, 1)
        attn = _forgetting_attention(heads(q), heads(k), heads(v), log_f)
        attn = attn.transpose(0, 2, 1, 3).reshape(bsz, seq, ATTN_WIDTH)
        conv = _conformer_conv(a, g, w_dw[layer], b_dw[layer], gn_g[layer], gn_b[layer])

        mixed = jnp.concatenate([_rmsnorm(attn, g_attn_out[layer]),
                                 _rmsnorm(conv, g_conv_out[layer])], axis=-1) @ w_out[layer]
        x = _layernorm(DEEPNORM_ALPHA * x + (1 + gt1) * mixed, ln1_g[layer], ln1_b[layer])

        u2 = x * (1 + sc2) + sh2
        hid = jnp.square(jax.nn.relu(u2 @ w_ff1[layer]))
        ff = hid @ w_ff2[layer]
        x = _layernorm(DEEPNORM_ALPHA * x + (1 + gt2) * ff, ln2_g[layer], ln2_b[layer])
    return x
```

```python
import numpy as np
import concourse.bass as bass
import concourse.mybir as mybir
from concourse.bass_utils import run_bass_kernel_spmd

F32 = mybir.dt.float32
BF16 = mybir.dt.bfloat16
AF = mybir.ActivationFunctionType
ALU = mybir.AluOpType
AX = mybir.AxisListType

D = 1024
S = 8192
KC = 8
NB = 64
NSLOT = 4
QB = 512
HALO = 32
TQ = NSLOT * QB
DFF = 4096
ALPHA = 2.0 ** 0.25
EPS = 1e-5
CLAMP = 75.0
IMMEDIATE = False
N_DMA_SEMS = 24
N_SW_SEMS = 8


class Sched:
    ENGS = ("pe", "act", "dve", "pool", "sp")

    def __init__(self):
        self.ins = []
        self.lw = {}
        self.rd = {}
        self.bar = set()
        self.since = set()

    def barrier(self):
        last = {}
        for i in self.since:
            I = self.ins[i]
            if I["dma"]:
                last[("dma", i)] = i
            else:
                last[I["eng"]] = max(last.get(I["eng"], -1), i)
        self.bar = set(last.values())
        self.since = set(self.bar)

    def add(self, eng, fn, reads=(), writes=(), dma=False):
        i = len(self.ins)
        deps = set(self.bar)
        raw = set()
        self.since.add(i)
        for k in reads:
            w = self.lw.get(k)
            if w:
                deps.update(w.values())
                raw.update(w.values())
        for k in writes:
            w = self.lw.get(k)
            if w:
                deps.update(w.values())
            r = self.rd.get(k)
            if r:
                deps.update(r.values())
        wtag = "dma" if dma else eng
        for k in writes:
            self.lw.setdefault(k, {})[wtag] = i
            self.rd[k] = {}
        tag = ("dma", i) if dma else eng
        for k in reads:
            self.rd.setdefault(k, {})[tag] = i
        self.ins.append(dict(eng=eng, fn=fn, deps=deps, raw=raw, dma=dma))
        return i

    def emit(self, nc):
        ins = self.ins
        need = []
        for i, I in enumerate(ins):
            best = {}
            dmas = set()
            for d in I["deps"]:
                P = ins[d]
                if P["dma"]:
                    dmas.add(d)
                    continue
                if P["eng"] == I["eng"] and not I["dma"]:
                    if I["eng"] == "pe":
                        continue
                    if d not in I["raw"]:
                        continue
                if best.get(P["eng"], -1) < d:
                    best[P["eng"]] = d
            need.append((best, dmas))
        flagged = set()
        for best, _ in need:
            flagged.update(best.values())
        cnt = {}
        run = {e: 0 for e in self.ENGS}
        for i, I in enumerate(ins):
            if I["dma"]:
                continue
            if i in flagged:
                run[I["eng"]] += 1
            cnt[i] = run[I["eng"]]
        dma_ids = [i for i, I in enumerate(ins) if I["dma"]]
        dma_sem = {}
        prev_dma = {}
        n_hw = N_DMA_SEMS - N_SW_SEMS
        for qname, base, width in (("sp", 0, n_hw), ("pool", n_hw, N_SW_SEMS)):
            ids = [i for i in dma_ids if ins[i]["eng"] == qname]
            for k, i in enumerate(ids):
                dma_sem[i] = (base + k % width, 16 * (k // width + 1))
                prev_dma[i] = ids[k - width] if k >= width else None
        per_eng = {e: [i for i, I in enumerate(ins) if I["eng"] == e] for e in self.ENGS}
        final_dma = {}
        for i in dma_ids:
            s, v = dma_sem[i]
            final_dma[s] = max(final_dma.get(s, 0), v)

        import contextlib
        with contextlib.ExitStack() as st:
            esem = {e: st.enter_context(nc.semaphore("sem_" + e)) for e in ("pe", "act", "dve", "pool")}
            dsem = [st.enter_context(nc.semaphore("sem_dma%d" % k)) for k in range(N_DMA_SEMS)]
            block = st.enter_context(nc.Block())

            def body(eng_name):
                def run_eng(eng):
                    waited = {}

                    def wait(key, sem, val):
                        if waited.get(key, 0) >= val:
                            return
                        eng.wait_ge(sem, val)
                        waited[key] = val

                    for i in per_eng[eng_name]:
                        I = ins[i]
                        best, dmas = need[i]
                        for pe_name, d in best.items():
                            wait(pe_name, esem[pe_name], cnt[d])
                        for d in dmas:
                            s, v = dma_sem[d]
                            wait(("d", s), dsem[s], v)
                        if I["dma"]:
                            p = prev_dma[i]
                            if p is not None:
                                s, v = dma_sem[p]
                                wait(("d", s), dsem[s], v)
                        ret = I["fn"](eng)
                        if I["dma"]:
                            s, v = dma_sem[i]
                            ret.then_inc(dsem[s], 16)
                        elif i in flagged:
                            ret.then_inc(esem[eng_name], 1)
                    if eng_name == "sp":
                        for s, v in sorted(final_dma.items()):
                            wait(("d", s), dsem[s], v)
                return run_eng

            block.tensor(body("pe"))
            block.scalar(body("act"))
            block.vector(body("dve"))
            block.gpsimd(body("pool"))
            block.sync(body("sp"))


def build_program(debug=None):
    nc = bass.Bass("TRN2", target_bir_lowering=False)
    sch = Sched()
    dbg = {}

    def din(name, shape, dt=F32):
        return nc.dram_tensor(name, list(shape), dt, kind="ExternalInput").ap()

    def dout(name, shape, dt=F32):
        return nc.dram_tensor(name, list(shape), dt, kind="ExternalOutput").ap()

    xT_d = din("xT", [16, 128, KC, 512])
    xqT_d = din("xqT", [NSLOT, 128, KC, HALO + QB])
    xq_d = din("xq", [16, 128, D])
    halo_d = din("halo_ok", [128, NSLOT])
    sel_d = din("sel", [128, NSLOT, NB, 8])
    mask_d = din("mask", [128, 16, QB])
    wk_d = din("wk", [128, KC, 512])
    wq_d = din("wq", [128, KC, 512])
    wvf_d = din("wvf", [128, KC, 2, 260])
    wa_d = din("wa", [128, KC, 512])
    wg_d = din("wg", [128, KC, 512])
    wo_d = din("wo", [128, KC, D])
    w1_d = din("w1", [128, KC, DFF])
    w2_d = din("w2", [128, 32, D])
    wada_d = din("wada", [128, KC, 6 * D])
    cT_d = din("cT", [128, KC])
    bada_d = din("bada", [1, 6 * D])
    bf_d = din("bfg", [128, 8])
    wdw_d = din("wdw", [128, 4, 31])
    cvec_d = din("cvec", [128, 4, 4])
    gao_d = din("gao", [128, 8])
    lnv_d = din("lnv", [4, 128, D])
    ident_d = din("ident", [128, 128])
    utri_d = din("utri", [128, 128])
    gmat_d = din("gmat", [128, 128])
    out_d = dout("out", [16, 128, D])

    import contextlib
    st = contextlib.ExitStack()

    def sb(name, shape, dt=F32):
        return st.enter_context(nc.sbuf_tensor("s_" + name, list(shape), dt))

    PS = [st.enter_context(nc.psum_tensor("ps%d" % k, [128, 512], F32)) for k in range(8)]

    def psk(k):
        return ("ps", k)

    def dma(eng, out, in_, reads, writes):
        if eng == "sp":
            sch.add("sp", lambda e: e.dma_start(out=out, in_=in_), reads, writes, dma=True)
        else:
            sch.add("pool", lambda e: e.dma_start(out=out, in_=in_), reads, writes, dma=True)

    def mm(out, lhsT, rhs, start, stop, reads, writes):
        sch.add("pe", lambda e: e.matmul(out, lhsT, rhs, start=start, stop=stop), reads, writes)

    def act(out, in_, func, reads, writes, bias=None, scale=None):
        kw = {}
        if bias is not None:
            kw["bias"] = bias
        if scale is not None:
            kw["scale"] = scale
        sch.add("act", lambda e: e.activation(out, in_, func, **kw), reads, writes)

    def ts(eng, out, in0, s1, s2, op0, op1, reads, writes):
        if op1 is None:
            sch.add(eng, lambda e: e.tensor_scalar(out, in0, s1, None, op0), reads, writes)
        else:
            sch.add(eng, lambda e: e.tensor_scalar(out, in0, s1, s2, op0, op1), reads, writes)

    def tt(eng, out, in0, in1, op, reads, writes):
        sch.add(eng, lambda e: e.tensor_tensor(out, in0, in1, op), reads, writes)

    def cp(eng, out, in_, reads, writes):
        if eng == "act":
            act(out, in_, AF.Copy, reads, writes)
        else:
            sch.add(eng, lambda e: e.tensor_copy(out, in_), reads, writes)

    def memset(eng, ap, val, writes):
        sch.add(eng, lambda e: e.memset(ap, val), (), writes)

    ident = sb("ident", [128, 128])
    utri = sb("utri", [128, 128])
    ones_f = sb("ones_f", [128, 128])
    dma("sp", ident[:], ident_d, (), ["ident"])
    dma("sp", utri[:], utri_d, (), ["utri"])
    memset("pool", ones_f[:], 1.0, ["ones_f"])

    cT = sb("cT", [128, KC])
    sc_c = sb("sc_c", [128, KC])
    pv = sb("pv", [128, 32])
    scp1 = sb("scp1", [128, 16])
    gt_scr = nc.dram_tensor("gt_scr", [128, 2, D], F32, kind="Internal").ap()
    bfg = sb("bfg", [128, 8])
    dma("sp", cT[:], cT_d, (), ["cT"])
    dma("sp", bfg[:], bf_d, (), ["bfg"])

    act(sc_c[:], cT[:], AF.Silu, ["cT"], ["sc_c"])
    with nc.sbuf_tensor("s_wada_buf", [128, 2, KC, 512], F32) as wab, \
            nc.sbuf_tensor("s_bada", [1, 6 * D], F32) as bada, \
            nc.sbuf_tensor("s_adarow", [1, 6 * D], F32) as adarow, \
            nc.sbuf_tensor("s_gtp1", [128, 2, D], F32) as gtp1:
        dma("sp", bada[:], bada_d, (), ["bada"])
        for q in range(12):
            b = q % 2
            dma("sp", wab[:, b], wada_d[:, :, q * 512:(q + 1) * 512], (), [("wab", b)])
            pk = q % 2
            for kc in range(KC):
                mm(PS[pk][0:1, :], sc_c[:, kc:kc + 1], wab[:, b, kc, :], kc == 0, kc == KC - 1,
                   ["sc_c", ("wab", b)], [psk(pk)])
            tt("dve", adarow[0:1, q * 512:(q + 1) * 512], PS[pk][0:1, :], bada[0:1, q * 512:(q + 1) * 512],
               ALU.add, [psk(pk), "bada"], ["adarow"])
        for vi, off in enumerate((0, D, 3 * D, 4 * D)):
            for kc in range(KC):
                col = vi * 8 + kc
                mm(PS[2][:, col:col + 1], adarow[0:1, off + kc * 128: off + (kc + 1) * 128], ones_f[0:1, 0:1],
                   True, True, ["adarow", "ones_f"], [psk(2)])
        cp("dve", pv[:], PS[2][:, 0:32], [psk(2)], ["pv"])
        ts("dve", scp1[:, 0:8], pv[:, 8:16], 1.0, None, ALU.add, None, ["pv"], ["scp1"])
        ts("dve", scp1[:, 8:16], pv[:, 24:32], 1.0, None, ALU.add, None, ["pv"], ["scp1"])
        for gi, off in enumerate((2 * D, 5 * D)):
            for hf in range(2):
                pk = 3 + hf
                mm(PS[pk][:, :], ones_f[0:1, 0:128], adarow[0:1, off + hf * 512: off + (hf + 1) * 512],
                   True, True, ["adarow", "ones_f"], [psk(pk)])
                ts("dve", gtp1[:, gi, hf * 512:(hf + 1) * 512], PS[pk][:, :], 1.0, None, ALU.add, None,
                   [psk(pk)], ["gtp1"])
        dma("sp", gt_scr, gtp1[:], ["gtp1"], ["gt_scr"])
    sch.barrier()
    sh1 = pv[:, 0:8]
    sh2 = pv[:, 16:24]

    if debug == "A":
        dbg["pv"] = dout("dbg_pv", [128, 32])
        dbg["gtp1"] = dout("dbg_gtp1", [128, 2, D])
        dma("sp", dbg["pv"], pv[:], ["pv", "scp1"], ["o1"])
        dma("sp", dbg["gtp1"], gt_scr, ["gt_scr"], ["o2"])
        sch.emit(nc)
        st.close()
        return nc

    R1 = sb("R1", [128, 6144])

    def view(ap, dt, pat=None, **kw):
        v = ap if dt == F32 else ap.bitcast(dt)
        return v.rearrange(pat, **kw) if pat else v

    ATn = view(R1[:, 0:4096], BF16, "p (a b) -> p a b", a=4)
    accA = R1[:, 4096:6144]
    stE = contextlib.ExitStack()

    def sbE(name, shape, dt=F32):
        return stE.enter_context(nc.sbuf_tensor("s_" + name, list(shape), dt))

    KT = sbE("KT", [128, 4, S], BF16)
    Vtf = sbE("Vt", [128, NB * 520 + 64], BF16)
    Vt = Vtf[:, 0:NB * 520].rearrange("p (b h d) -> p b h d", b=NB, h=8)
    fl = sbE("fl", [128, NB, 8])
    memset("pool", Vt[:, :, :, 64:65], 1.0, ["Vones"])
    memset("pool", Vtf[:, NB * 520:NB * 520 + 64], 0.0, ["Vones"])

    with contextlib.ExitStack() as stB:
        def sbB(name, shape, dt=F32):
            return stB.enter_context(nc.sbuf_tensor("s_" + name, list(shape), dt))
        wk = sbB("wk", [128, KC, 512], BF16)
        wvf = sbB("wvf", [128, KC, 2, 260], BF16)
        xs = sbB("xs", [128, 1, KC, QB])
        uT = sbB("uT", [128, 2, KC, QB], BF16)
        dma("pool", wk[:], wk_d, (), ["wk"])
        dma("pool", wvf[:], wvf_d, (), ["wvf"])

        def make_u(xs, uT, buf, width, it, xbuf=None):
            for kc in range(KC):
                eng = ("pool", "dve")[(kc + it) % 2] if kc % 4 != 3 else "act"
                o = uT[:, buf, kc, 0:width]
                xb = buf if xbuf is None else xbuf
                i_ = xs[:, xb, kc, 0:width]
                rk = [("xs", xb, kc // 4), "scp1", "pv"]
                wkk = [("uT", buf, kc)]
                if eng == "act":
                    act(o, i_, AF.Identity, rk, wkk, bias=sh1[:, kc:kc + 1], scale=scp1[:, kc:kc + 1])
                else:
                    ts(eng, o, i_, scp1[:, kc:kc + 1], sh1[:, kc:kc + 1], ALU.mult, ALU.add, rk, wkk)

        it = 0
        for c in range(16):
            buf = it % 2
            dma("sp", xs[:, 0, 0:4, :], xT_d[c, :, 0:4, :], (), [("xs", 0, 0)])
            dma("sp", xs[:, 0, 4:8, :], xT_d[c, :, 4:8, :], (), [("xs", 0, 1)])
            make_u(xs, uT, buf, 512, it, xbuf=0)
            for p in range(4):
                pk = p
                for kc in range(KC):
                    mm(PS[pk][:, :], wk[:, kc, p * 128:(p + 1) * 128], uT[:, buf, kc, 0:512], kc == 0, kc == KC - 1,
                       ["wk", ("uT", buf, kc)], [psk(pk)])
                cp("act", KT[:, p, c * 512:(c + 1) * 512], PS[pk][:, :], [psk(pk)], [("KT", p, c)])
            for tb in range(4):
                blk = c * 4 + tb
                for hf in range(2):
                    pk = 4 + (tb * 2 + hf) % 4
                    for kc in range(KC):
                        mm(PS[pk][:, 0:260], uT[:, buf, kc, tb * 128:(tb + 1) * 128], wvf[:, kc, hf, :],
                           kc == 0, kc == KC - 1, ["wvf", ("uT", buf, kc)], [psk(pk)])
                    cp("dve", Vt[:, blk, hf * 4:(hf + 1) * 4, 0:64],
                       PS[pk][:, 0:256].rearrange("p (h d) -> p h d", h=4), [psk(pk)], [("Vt", c)])
                    cp("dve", fl[:, blk, hf * 4:(hf + 1) * 4], PS[pk][:, 256:260], [psk(pk)], ["fl"])
            it += 1
    sch.barrier()
    QT = sbE("QT", [128, 4, TQ], BF16)
    with contextlib.ExitStack() as stB:
        wq = sbB("wq", [128, KC, 512], BF16)
        xs = sbB("xs2", [128, 1, KC, HALO + QB])
        uT = sbB("uT2", [128, 1, KC, HALO + QB], BF16)
        dma("pool", wq[:], wq_d, (), ["wq"])
        for m in range(NSLOT):
            buf = 0
            dma("sp", xs[:, buf], xqT_d[m], (), [("xs", buf, 0), ("xs", buf, 1)])
            make_u(xs, uT, buf, HALO + QB, it)
            for p in range(4):
                pk = p
                for kc in range(KC):
                    mm(PS[pk][:, :], wq[:, kc, p * 128:(p + 1) * 128], uT[:, buf, kc, HALO:HALO + QB],
                       kc == 0, kc == KC - 1, ["wq", ("uT", buf, kc)], [psk(pk)])
                cp("act", QT[:, p, m * QB:(m + 1) * QB], PS[pk][:, :], [psk(pk)], [("QT", p, m)])
            it += 1

    sch.barrier()
    if debug == "B":
        dbg["KT"] = dout("dbg_KT", [128, 4, S], BF16)
        dbg["Vt"] = dout("dbg_Vt", [128, NB, 8, 65], BF16)
        dbg["QT"] = dout("dbg_QT", [128, 4, TQ], BF16)
        dbg["fl"] = dout("dbg_fl", [128, NB, 8])
        dma("sp", dbg["KT"], KT[:], [("KT", p, c) for p in range(4) for c in range(16)], ["o1"])
        dma("sp", dbg["Vt"], Vt, [("Vt", c) for c in range(16)] + ["Vones"], ["o2"])
        dma("sp", dbg["QT"], QT[:], [("QT", p, m) for p in range(4) for m in range(4)], ["o3"])
        dma("sp", dbg["fl"], fl[:], ["fl"], ["o4"])
        sch.emit(nc)
        return nc

    bias = sbE("bias", [128, 8, NSLOT, NB])
    with contextlib.ExitStack() as stB:
        nbf = sbB("nbf", [128, 8])
        ee = sbB("ee", [128, NB, 8])
        lg = sbB("lg", [128, NB, 8])
        Ta = sbB("Ta", [128, NB, 8])
        Tb = sbB("Tb", [128, NB, 8])
        Tc = sbB("Tc", [128, NB, 8])
        G_tm = sbB("G_tm", [128, NB, 8])
        sel4 = sbB("sel4", [128, NSLOT, NB, 8])
        prod = sbB("prod", [128, NB, 8])
        Gref = sbB("Gref", [128, NSLOT, 8])
        dma("sp", sel4[:], sel_d, (), ["sel4"])
        if debug == "D0":
            dbg["G"] = dout("dbg_G", [128, NB, 8])
            dma("sp", dbg["G"], fl[:], ["fl"], ["o1"])
            sch.emit(nc)
            return nc
        ts("dve", nbf[:], bfg[:], -1.0, None, ALU.mult, None, ["bfg"], ["nbf"])
        for h in range(8):
            act(ee[:, :, h], fl[:, :, h], AF.Exp, ["fl", "nbf"], ["ee"], bias=nbf[:, h:h + 1], scale=-1.0)
        act(lg[:].rearrange("p b h -> p (b h)"), ee[:].rearrange("p b h -> p (b h)"), AF.Ln, ["ee"], ["lg"],
            bias=1.0, scale=1.0)

        if debug == "D1":
            dbg["G"] = dout("dbg_G", [128, NB, 8])
            dma("sp", dbg["G"], lg[:], ["lg"], ["o1"])
            sch.emit(nc)
            return nc
        lg2 = lg[:].rearrange("p b h -> p (b h)")
        mm(PS[0][:, :], utri[:], lg2, True, True, ["utri", "lg"], [psk(0)])
        mm(PS[1][:, :], ones_f[:], lg2, True, True, ["ones_f", "lg"], [psk(1)])
        cp("dve", Ta[:].rearrange("p b h -> p (b h)"), PS[1][:, :], [psk(1)], ["Ta"])
        cp("dve", Tc[:].rearrange("p b h -> p (b h)"), PS[1][:, :], [psk(1)], ["Tc"])

        if debug == "D2":
            dbg["G"] = dout("dbg_G", [128, NB, 8])
            dma("sp", dbg["G"], Tc[:], ["Tc"], ["o1"])
            sch.emit(nc)
            return nc
        src, dst, sk, dk = Ta, Tb, "Ta", "Tb"
        for dd in (1, 2, 4, 8, 16, 32):
            tt("dve", dst[:, dd:, :], src[:, dd:, :], src[:, :NB - dd, :], ALU.add, [sk], [dk])
            cp("dve", dst[:, :dd, :], src[:, :dd, :], [sk], [dk])
            src, dst, sk, dk = dst, src, dk, sk
        Pinc, pk_ = src, sk

        if debug == "D3":
            dbg["G"] = dout("dbg_G", [128, NB, 8])
            dma("sp", dbg["G"], Pinc[:], [pk_], ["o1"])
            sch.emit(nc)
            return nc
        tt("dve", Tc[:], Pinc[:], Tc[:], ALU.subtract, [pk_, "Tc"], ["Tc"])
        tt("dve", G_tm[:].rearrange("p b h -> p (b h)"), PS[0][:, :], Tc[:].rearrange("p b h -> p (b h)"),
           ALU.add, [psk(0), "Tc"], ["G_tm"])

        if debug == "D4":
            dbg["G"] = dout("dbg_G", [128, NB, 8])
            dma("sp", dbg["G"], G_tm[:], ["G_tm"], ["o1"])
            sch.emit(nc)
            return nc
        for m in range(NSLOT):
            tt("dve", prod[:], Pinc[:], sel4[:, m], ALU.mult, [pk_, "sel4"], ["prod"])
            sch.add("dve", (lambda m=m: (lambda e: e.tensor_reduce(
                Gref[:, m, :], prod[:].rearrange("p b h -> p h b"), AX.X, ALU.add)))(),
                ["prod"], ["Gref"])
        if debug == "D5":
            dbg["G"] = dout("dbg_G", [128, NSLOT, 8])
            dma("sp", dbg["G"], Gref[:], ["Gref"], ["o1"])
            sch.emit(nc)
            return nc
        for h in range(8):
            for m in range(NSLOT):
                ts("dve", bias[:, h, m, :], G_tm[:, :, h], Gref[:, m, h:h + 1], CLAMP, ALU.subtract, ALU.min,
                   ["G_tm", "Gref"], ["bias"])
        if debug == "D":
            dbg["G"] = dout("dbg_G", [128, NB, 8])
            dbg["bias"] = dout("dbg_bias", [128, 8, NSLOT, NB])
            dma("sp", dbg["G"], G_tm[:], ["G_tm"], ["o1"])
            dma("sp", dbg["bias"], bias[:], ["bias"], ["o2"])
            sch.emit(nc)
            return nc
    sch.barrier()

    maskt = sbE("maskt", [128, 16, QB], BF16)
    gao = sbE("gao", [128, 8])
    Pt = sbE("Pt", [128, 3, QB], BF16)
    Ot = sbE("Ot", [128, 1, QB])
    af = sbE("af", [128, QB])
    sqs = sbE("sqs", [128, QB])
    rl = sqs
    atsc = sbE("atsc", [128, QB], BF16)
    dma("pool", maskt[:], mask_d, (), ["mask"])
    dma("sp", gao[:], gao_d, (), ["gao"])
    ug = 0
    pm = 0
    pending = []

    def make_epilogue(p, m, obank):
        qs = slice(m * QB, (m + 1) * QB)
        stages = []
        for e in (0, 1):
            h = 2 * p + e
            ob = obank[e]

            def s1(ob=ob):
                cp("dve", Ot[0:65, 0, :], PS[ob][0:65, :], [psk(ob)], ["Ot"])

            def s2():
                sch.add("dve", lambda en: en.reciprocal(rl[64:65, :], Ot[64:65, 0, :]), ["Ot"], ["sqs"])

            def s3(ob=ob):
                mm(PS[ob][0:64, :], ones_f[64:65, 0:64], rl[64:65, :], True, True, ["sqs", "ones_f", "Ot"], [psk(ob)])

            def s4(ob=ob):
                tt("dve", af[0:64, :], Ot[0:64, 0, :], PS[ob][0:64, :], ALU.mult, ["Ot", psk(ob)], ["af"])

            def s5(e=e, h=h):
                if e == 0:
                    ts("dve", ATn[0:64, p, qs], af[0:64, :], gao[0:64, h:h + 1], None, ALU.mult, None,
                       ["af", "gao"], [("ATn", p, m, 0)])
                else:
                    ts("dve", atsc[0:64, :], af[0:64, :], gao[0:64, h:h + 1], None, ALU.mult, None,
                       ["af", "gao"], ["atsc"])
                    dma("sp", ATn[64:128, p, qs], atsc[0:64, :], ["atsc"], [("ATn", p, m, 1)])

            def s6(h=h):
                if h == 0:
                    tt("pool", accA[0:64, qs], af[0:64, :], af[0:64, :], ALU.mult, ["af"], [("accA", m)])
                else:
                    tt("pool", sqs[0:64, :], af[0:64, :], af[0:64, :], ALU.mult, ["af"], ["sqs"])
                    tt("pool", accA[0:64, qs], accA[0:64, qs], sqs[0:64, :], ALU.add, ["sqs", ("accA", m)],
                       [("accA", m)])
            stages += [s1, s2, s3, s4, s5, s6]
        return stages

    for p in range(4):
        for m in range(NSLOT):
            nkb = 16 * m + 16
            obank = [4 + 2 * (pm % 2), 5 + 2 * (pm % 2)]
            qs = slice(m * QB, (m + 1) * QB)

            def qkpair(kb):
                for e in (0, 1):
                    sbk = 2 * (kb % 2) + e
                    mm(PS[sbk][:, :], KT[e * 64:(e + 1) * 64, p, kb * 128:(kb + 1) * 128],
                       QT[e * 64:(e + 1) * 64, p, qs], True, True,
                       [("KT", p, kb // 4), ("QT", p, m)], [psk(sbk)])

            qkpair(0)
            for kb in range(nkb):
                if kb + 1 < nkb:
                    qkpair(kb + 1)
                for e in (0, 1):
                    h = 2 * p + e
                    sbk = 2 * (kb % 2) + e
                    pi = (ug + 2 * kb + e) % 3
                    act(Pt[:, pi, :], PS[sbk][:, :], AF.Exp, [psk(sbk), "bias"], [("Pt", pi)],
                        bias=bias[:, h, m, kb:kb + 1], scale=0.125)
                    if kb >= 16 * m:
                        tt("dve", Pt[:, pi, :], Pt[:, pi, :], maskt[:, kb - 16 * m, :], ALU.mult,
                           [("Pt", pi), "mask"], [("Pt", pi)])
                    v0 = kb * 520 + h * 65
                    mm(PS[obank[e]][:, :], Vtf[:, v0:v0 + 128], Pt[:, pi, :], kb == 0, kb == nkb - 1,
                       [("Vt", kb // 4), "Vones", ("Pt", pi)], [psk(obank[e])])
                if pending:
                    pending.pop(0)()
            while pending:
                pending.pop(0)()
            ug += 2 * nkb
            pending = make_epilogue(p, m, obank)
            if IMMEDIATE:
                while pending:
                    pending.pop(0)()
            pm += 1
    while pending:
        pending.pop(0)()
    if debug == "E":
        dbg["ATn"] = dout("dbg_ATn", [128, 4, TQ], BF16)
        dbg["accA"] = dout("dbg_accA", [128, TQ])
        dma("sp", dbg["ATn"], ATn, [("ATn", p, m, e) for p in range(4) for m in range(4) for e in (0, 1)], ["o1"])
        dma("sp", dbg["accA"], accA, [("accA", m) for m in range(4)], ["o2"])
        sch.emit(nc)
        return nc
    stE.close()
    sch.barrier()


    x1_scr = nc.dram_tensor("x1_scr", [16, 128, D], F32, kind="Internal").ap()
    R2 = sb("R2", [128, 6144])
    CT = view(R2[:, 0:4096], BF16, "p (a b) -> p a b", a=4)
    accC = R2[:, 4096:6144]
    with contextlib.ExitStack() as stB:
        wa = sbB("wa", [128, KC, 512], BF16)
        wg = sbB("wg", [128, KC, 512], BF16)
        xs = sbB("xs3", [128, 1, KC, HALO + QB])
        uT = sbB("uT3", [128, 2, KC, HALO + QB], BF16)
        ucv = sbB("ucv", [128, 2, 4, HALO + QB], BF16)
        diag = sbB("diag", [128, 4, 31, 128], BF16)
        wdw = sbB("wdw", [128, 4, 31])
        cvec = sbB("cvec", [128, 4, 4])
        halo = sbB("halo", [128, NSLOT])
        gmat = sbB("gmat", [128, 128])
        sig = sbB("sig", [128, 2, HALO + QB])
        ysb = sbB("ysb", [128, 4, QB])
        dsb = sbB("dsb", [128, 4, QB])
        sq4 = sbB("sq4", [128, 2, QB])
        lnv_ = sbB("lnv_", [128, 2, QB])
        rstd = sbB("rstd", [128, 4, QB])
        co = sbB("co", [128, 2, QB])
        csq = sbB("csq", [128, QB])
        dma("pool", wa[:], wa_d, (), ["wa"])
        dma("pool", wg[:], wg_d, (), ["wg"])
        dma("sp", wdw[:], wdw_d, (), ["wdw"])
        dma("sp", cvec[:], cvec_d, (), ["cvec"])
        dma("sp", halo[:], halo_d, (), ["halo"])
        dma("sp", gmat[:], gmat_d, (), ["gmat"])
        W = HALO + QB

        def f1a(m):
            dma("sp", xs[:, 0], xqT_d[m], (), [("xs", 0, 0), ("xs", 0, 1)])
            make_u(xs, uT, m % 2, W, m, xbuf=0)

        def f1(m):
            ub = m % 2
            for cc in range(4):
                pa, pg = cc % 2, 2 + cc % 2
                for (ps_, w_, nm) in ((pa, wa, "wa"), (pg, wg, "wg")):
                    for kc in range(KC):
                        mm(PS[ps_][:, :], w_[:, kc, cc * 128:(cc + 1) * 128], uT[:, ub, kc, HALO:W],
                           kc == 0, kc == KC - 1, [nm, ("uT", ub, kc)], [psk(ps_)])
                    for kc in range(KC):
                        mm(PS[4 + ps_][:, 0:HALO], w_[:, kc, cc * 128:(cc + 1) * 128], uT[:, ub, kc, 0:HALO],
                           kc == 0, kc == KC - 1, [nm, ("uT", ub, kc)], [psk(4 + ps_)])
                sb_ = cc % 2
                act(sig[:, sb_, HALO:W], PS[pg][:, :], AF.Sigmoid, [psk(pg)], [("sig", sb_)])
                act(sig[:, sb_, 0:HALO], PS[4 + pg][:, 0:HALO], AF.Sigmoid, [psk(4 + pg)], [("sig", sb_)])
                tt("dve", ucv[:, ub, cc, HALO:W], PS[pa][:, :], sig[:, sb_, HALO:W], ALU.mult,
                   [psk(pa), ("sig", sb_)], [("ucv", ub, cc)])
                sch.add("dve", (lambda cc=cc, pa=pa, sb_=sb_, m=m, ub=ub: (lambda e: e.scalar_tensor_tensor(
                    ucv[:, ub, cc, 0:HALO], PS[4 + pa][:, 0:HALO], halo[:, m:m + 1], sig[:, sb_, 0:HALO],
                    ALU.mult, ALU.mult)))(), [psk(4 + pa), ("sig", sb_), "halo"], [("ucv", ub, cc)])
                drain_diag(16)

        def f2(m):
            ub = m % 2
            qs = slice(m * QB, (m + 1) * QB)
            for cc in range(4):
                pk = cc % 2
                for k in range(31):
                    mm(PS[pk][:, :], diag[:, cc, k, :], ucv[:, ub, cc, 2 + k: 2 + k + QB], k == 0, k == 30,
                       [("diag", cc), ("ucv", ub, cc)], [psk(pk)])
                ts("dve", ysb[:, cc, :], PS[pk][:, :], cvec[:, cc, 0:1], None, ALU.add, None,
                   [psk(pk), "cvec"], [("ysb", cc)])
            for cc in range(4):
                pm_, pv_ = 2 + cc % 2, 4 + cc % 2
                mm(PS[pm_][:, :], gmat[:], ysb[:, cc, :], True, True, ["gmat", ("ysb", cc)], [psk(pm_)])
                tt("dve", dsb[:, cc, :], ysb[:, cc, :], PS[pm_][:, :], ALU.subtract, [("ysb", cc), psk(pm_)],
                   [("dsb", cc)])
                tt("dve", sq4[:, cc % 2, :], dsb[:, cc, :], dsb[:, cc, :], ALU.mult, [("dsb", cc)], [("sq4", cc % 2)])
                mm(PS[pv_][:, :], gmat[:], sq4[:, cc % 2, :], True, True, ["gmat", ("sq4", cc % 2)], [psk(pv_)])
                act(lnv_[:, cc % 2, :], PS[pv_][:, :], AF.Ln, [psk(pv_)], [("lnv_", cc % 2)], bias=EPS, scale=1.0)
                act(rstd[:, cc, :], lnv_[:, cc % 2, :], AF.Exp, [("lnv_", cc % 2)], [("rstd", cc)], scale=-0.5)
                tt("dve", dsb[:, cc, :], dsb[:, cc, :], rstd[:, cc, :], ALU.mult, [("dsb", cc), ("rstd", cc)],
                   [("dsb", cc)])
            for cc in range(4):
                cb = cc % 2
                act(co[:, cb, :], dsb[:, cc, :], AF.Silu, [("dsb", cc), "cvec"], [("co", cb)],
                    bias=cvec[:, cc, 2:3], scale=cvec[:, cc, 1:2])
                ts("dve", CT[:, cc, qs], co[:, cb, :], cvec[:, cc, 3:4], None, ALU.mult, None,
                   [("co", cb), "cvec"], [("CT", cc, m)])
                if cc == 0:
                    tt("dve", accC[:, qs], co[:, cb, :], co[:, cb, :], ALU.mult, [("co", cb)], [("accC", m)])
                else:
                    tt("dve", csq[:], co[:, cb, :], co[:, cb, :], ALU.mult, [("co", cb)], ["csq"])
                    tt("dve", accC[:, qs], accC[:, qs], csq[:], ALU.add, ["csq", ("accC", m)], [("accC", m)])

        diag_ops = []

        def mk_diag(cc, k):
            def go():
                if k % 2 == 0:
                    ts("dve", diag[:, cc, k, :], ident[:], wdw[:, cc, k:k + 1], None, ALU.mult, None,
                       ["ident", "wdw"], [("diag", cc)])
                else:
                    act(diag[:, cc, k, :], ident[:], AF.Identity, ["ident", "wdw"], [("diag", cc)],
                        scale=wdw[:, cc, k:k + 1], bias=0.0)
            return go

        for cc in range(4):
            for k in range(31):
                diag_ops.append(mk_diag(cc, k))

        def drain_diag(n):
            for _ in range(n):
                if diag_ops:
                    diag_ops.pop(0)()

        f1a(0)
        f1a(1)
        f1(0)
        for m in range(NSLOT):
            if m + 1 < NSLOT:
                f1(m + 1)
                if m + 2 < NSLOT:
                    f1a(m + 2)
            drain_diag(1000)
            f2(m)
        if debug == "F":
            dbg["CT"] = dout("dbg_CT", [128, 4, TQ], BF16)
            dbg["accC"] = dout("dbg_accC", [128, TQ])
            dma("sp", dbg["CT"], CT, [("CT", cc, m) for cc in range(4) for m in range(4)], ["o1"])
            dma("sp", dbg["accC"], accC, [("accC", m) for m in range(4)], ["o2"])
            sch.emit(nc)
            return nc
    sch.barrier()

    w1 = sb("w1", [128, KC, DFF], BF16)
    rsd = sb("rsd", [128, 2, 16])
    with contextlib.ExitStack() as stB:
        wo = sbB("wo", [128, KC, D], BF16)
        ln1 = sbB("ln1", [128, 2, D])
        gt1 = sbB("gt1", [128, D])
        xqt = sbB("xqt", [128, 8, D])
        tA = sbB("tA", [128, 8, D])
        st6 = sbB("st6", [128, 2, 6])
        mv = sbB("mv", [128, 4])
        lnr = sbB("lnr", [128, 2, 16])
        dma("pool", wo[:], wo_d, (), ["wo"])
        for kc in range(KC):
            dma("pool", w1[:, kc, :], w1_d[:, kc, :], (), [("w1", kc)])
        dma("sp", ln1[:, 0, :], lnv_d[0], (), ["ln1"])
        dma("sp", ln1[:, 1, :], lnv_d[1], (), ["ln1"])
        dma("sp", gt1[:], gt_scr[:, 0, :], ["gt_scr"], ["gt1"])
        for kc in range(KC):
            tt(("dve", "pool")[kc % 2], wo[:, kc, :], wo[:, kc, :], gt1[:], ALU.mult, ["wo", "gt1"], ["wo"])
        for t in range(16):
            mm(PS[6][:, t:t + 1], accA[0:64, t * 128:(t + 1) * 128], ones_f[0:64, 0:1], True, True,
               [("accA", t // 4), "ones_f"], [psk(6)])
        for t in range(16):
            mm(PS[6][:, 16 + t:17 + t], accC[:, t * 128:(t + 1) * 128], ones_f[:, 0:1], True, True,
               [("accC", t // 4), "ones_f"], [psk(6)])
        act(lnr[:].rearrange("p a b -> p (a b)"), PS[6][:, 0:32], AF.Ln, [psk(6)], ["lnr"], bias=EPS, scale=1.0 / 512)
        act(rsd[:].rearrange("p a b -> p (a b)"), lnr[:].rearrange("p a b -> p (a b)"), AF.Exp, ["lnr"], ["rsd"],
            scale=-0.5)

        def layer_norm(src, dst, gam, bet, key_src, key_dst, gkey, st6, mv):
            for hf in range(2):
                sch.add("dve", (lambda hf=hf: (lambda e: e.bn_stats(st6[:, hf, :], src[:, hf * 512:(hf + 1) * 512])))(),
                        [key_src], ["st6"])
            sch.add("dve", lambda e: e.bn_aggr(mv[:, 0:2], st6[:].rearrange("p a b -> p (a b)")), ["st6"], ["mv"])
            act(mv[:, 2:3], mv[:, 1:2], AF.Ln, ["mv"], ["mv2"], bias=EPS, scale=1.0)
            act(mv[:, 2:3], mv[:, 2:3], AF.Exp, ["mv2"], ["mv2"], scale=-0.5)
            ts("dve", mv[:, 3:4], mv[:, 0:1], -1.0, mv[:, 2:3], ALU.mult, ALU.mult, ["mv", "mv2"], ["mv3"])
            act(dst, src, AF.Identity, [key_src, "mv2", "mv3"], [key_dst], bias=mv[:, 3:4], scale=mv[:, 2:3])
            tt("dve", dst, dst, gam, ALU.mult, [key_dst, gkey], [key_dst])
            tt("dve", dst, dst, bet, ALU.add, [key_dst, gkey], [key_dst])

        st64 = sbB("st64", [128, 8, 2, 6])
        mv4 = sbB("mv4", [128, 8, 4])

        def g_p1(t0):
            tiles = [t for t in (t0, t0 + 1) if t < 16]
            for t in tiles:
                b4 = t % 8
                pb = 4 * (t % 2)
                tsl = slice(t * 128, (t + 1) * 128)
                dma("sp", xqt[:, b4, :], xq_d[t], (), [("xqt", b4)])
                for hf in range(2):
                    for p in range(4):
                        mm(PS[pb + hf][:, :], ATn[:, p, tsl], wo[:, p, hf * 512:(hf + 1) * 512], p == 0, p == 3,
                           [("ATn", p, t // 4, 0), ("ATn", p, t // 4, 1), "wo"], [psk(pb + hf)])
                    for cc in range(4):
                        mm(PS[pb + 2 + hf][:, :], CT[:, cc, tsl], wo[:, 4 + cc, hf * 512:(hf + 1) * 512], cc == 0, cc == 3,
                           [("CT", cc, t // 4), "wo"], [psk(pb + 2 + hf)])
            for hf in range(2):
                hs = slice(hf * 512, (hf + 1) * 512)
                for t in tiles:
                    b4 = t % 8
                    pb = 4 * (t % 2)
                    act(tA[:, b4, hs], PS[pb + hf][:, :], AF.Identity, [psk(pb + hf), "rsd"], [("tA", b4)],
                        scale=rsd[:, 0, t:t + 1], bias=0.0)
                for t in tiles:
                    b4 = t % 8
                    pb = 4 * (t % 2)
                    sch.add("dve", (lambda hf=hf, hs=hs, b4=b4, t=t, pb=pb: (lambda e: e.scalar_tensor_tensor(
                        tA[:, b4, hs], PS[pb + 2 + hf][:, :], rsd[:, 1, t:t + 1], tA[:, b4, hs], ALU.mult, ALU.add)))(),
                        [psk(pb + 2 + hf), "rsd", ("tA", b4)], [("tA", b4)])
            for t in tiles:
                b4 = t % 8
                sch.add("dve", (lambda b4=b4: (lambda e: e.scalar_tensor_tensor(
                    xqt[:, b4, :], xqt[:, b4, :], ALPHA, tA[:, b4, :], ALU.mult, ALU.add)))(),
                    [("xqt", b4), ("tA", b4)], [("xqt", b4)])

        def g_lna(t):
            b4 = t % 8
            for hf in range(2):
                sch.add("dve", (lambda hf=hf, b4=b4: (lambda e: e.bn_stats(
                    st64[:, b4, hf, :], xqt[:, b4, hf * 512:(hf + 1) * 512])))(), [("xqt", b4)], [("st6", b4)])
            sch.add("dve", (lambda b4=b4: (lambda e: e.bn_aggr(
                mv4[:, b4, 0:2], st64[:, b4].rearrange("p a b -> p (a b)"))))(), [("st6", b4)], [("mv", b4)])
            act(mv4[:, b4, 2:3], mv4[:, b4, 1:2], AF.Ln, [("mv", b4)], [("mv2", b4)], bias=EPS, scale=1.0)
            act(mv4[:, b4, 2:3], mv4[:, b4, 2:3], AF.Exp, [("mv2", b4)], [("mv2", b4)], scale=-0.5)

        def g_lnb(t):
            b4 = t % 8
            ts("dve", mv4[:, b4, 3:4], mv4[:, b4, 0:1], -1.0, mv4[:, b4, 2:3], ALU.mult, ALU.mult,
               [("mv", b4), ("mv2", b4)], [("mv3", b4)])
            act(tA[:, b4, :], xqt[:, b4, :], AF.Identity, [("xqt", b4), ("mv2", b4), ("mv3", b4)], [("tA", b4)],
                bias=mv4[:, b4, 3:4], scale=mv4[:, b4, 2:3])

        def g_lnc(t):
            b4 = t % 8
            tt("pool", tA[:, b4, :], tA[:, b4, :], ln1[:, 0, :], ALU.mult, [("tA", b4), "ln1"], [("tA", b4)])
            tt("pool", tA[:, b4, :], tA[:, b4, :], ln1[:, 1, :], ALU.add, [("tA", b4), "ln1"], [("tA", b4)])
            dma("sp", x1_scr[t], tA[:, b4, :], [("tA", b4)], [("x1", t)])

        def pair(fn, t0):
            for t in (t0, t0 + 1):
                if 0 <= t < 16:
                    fn(t)

        for it_ in range(0, 16 + 6, 2):
            if 0 <= it_ - 4 < 16:
                pair(g_lnb, it_ - 4)
            if it_ < 16:
                g_p1(it_)
            if 0 <= it_ - 2 < 16:
                pair(g_lna, it_ - 2)
            if 0 <= it_ - 6 < 16:
                pair(g_lnc, it_ - 6)
        if debug == "G":
            dbg["x1"] = dout("dbg_x1", [16, 128, D])
            dma("sp", dbg["x1"], x1_scr, [("x1", t) for t in range(16)], ["o1"])
            sch.emit(nc)
            return nc
    sch.barrier()

    with contextlib.ExitStack() as stB:
        w2 = sbB("w2", [128, 32, D], BF16)
        for q in range(4):
            dma("pool", w2[:, q * 8:(q + 1) * 8, :], w2_d[:, q * 8:(q + 1) * 8, :], (), [("w2", q)])
        ln2 = sbB("ln2", [128, 2, D])
        gt2 = sbB("gt2", [128, D])
        st6b = sbB("st6b", [128, 2, 6])
        mvb = sbB("mvb", [128, 4])
        dma("sp", ln2[:, 0, :], lnv_d[2], (), ["ln2"])
        dma("sp", ln2[:, 1, :], lnv_d[3], (), ["ln2"])
        dma("sp", gt2[:], gt_scr[:, 1, :], ["gt_scr"], ["gt2"])
        x1t = R1[:, 0:4096].rearrange("p (a b) -> p a b", a=4)
        hTp = view(R1[:, 4096:6144], BF16, "p (a b c) -> p a b c", a=2, b=8)
        u2g = view(R2[:, 0:2048], BF16, "p (a b c) -> p a b c", a=2, b=8)
        trl = R2[:, 2048:2560].rearrange("p (a b) -> p a b", a=2)
        t2 = R2[:, 2560:4608].rearrange("p (a b) -> p a b", a=2)
        sh2p = pv[:, 16:24]
        ui = 0
        def prep_load(g):
            gb = g % 2
            for tb in range(2):
                t = 2 * g + tb
                dma("sp", x1t[:, gb * 2 + tb, :], x1_scr[t], [("x1", t)], [("x1t", gb, tb)])

        def prep_batch(g, bi):
            gb = g % 2
            for tb in (bi // 2,):
                for kq in (bi % 2,):
                    for k4 in range(4):
                        kc = kq * 4 + k4
                        sch.add("pe", (lambda gb=gb, tb=tb, kc=kc, k4=k4: (lambda e: e.transpose(
                            PS[7][:, k4 * 128:(k4 + 1) * 128], x1t[:, gb * 2 + tb, kc * 128:(kc + 1) * 128], ident[:])))(),
                            [("x1t", gb, tb), "ident"], [psk(7)])
                    for k4 in range(4):
                        kc = kq * 4 + k4
                        o_ = u2g[:, gb, kc, tb * 128:(tb + 1) * 128]
                        i_ = PS[7][:, k4 * 128:(k4 + 1) * 128]
                        if k4 % 2 == 0:
                            act(o_, i_, AF.Identity, [psk(7), "scp1", "pv"], [("u2g", gb)],
                                bias=sh2p[:, kc:kc + 1], scale=scp1[:, 8 + kc:9 + kc])
                        else:
                            ts("dve", o_, i_, scp1[:, 8 + kc:9 + kc], sh2p[:, kc:kc + 1], ALU.mult, ALU.add,
                               [psk(7), "scp1", "pv"], [("u2g", gb)])

        ui_box = [0]

        def up(g, q, extra=None):
            gb = g % 2
            for fcl in range(8):
                if extra is not None and fcl % 2 == 1:
                    prep_batch(extra, fcl // 2)
                ui = ui_box[0]
                fc = 8 * q + fcl
                pk = ui % 3
                for kc in range(KC):
                    mm(PS[pk][:, 0:256], w1[:, kc, fc * 128:(fc + 1) * 128], u2g[:, gb, kc, :], kc == 0, kc == KC - 1,
                       [("w1", kc), ("u2g", gb)], [psk(pk)])
                tb_ = ui % 2
                act(trl[:, tb_, :], PS[pk][:, 0:256], AF.Relu, [psk(pk)], [("trl", tb_)])
                tt("dve", hTp[:, q % 2, fcl, :], trl[:, tb_, :], trl[:, tb_, :], ALU.mult, [("trl", tb_)],
                   [("hTp", q % 2, fcl)])
                ui_box[0] += 1

        def down(g, q):
            for tb in range(2):
                for hf in range(2):
                    pk = 3 + tb * 2 + hf
                    for fcl in range(8):
                        fc = 8 * q + fcl
                        mm(PS[pk][:, :], hTp[:, q % 2, fcl, tb * 128:(tb + 1) * 128], w2[:, fc, hf * 512:(hf + 1) * 512],
                           q == 0 and fcl == 0, q == 3 and fcl == 7, [("hTp", q % 2, fcl), ("w2", q)], [psk(pk)])

        def epilogue(g):
            gb = g % 2
            for tb in range(2):
                t = 2 * g + tb
                xt = x1t[:, gb * 2 + tb, :]
                for hf in range(2):
                    hs = slice(hf * 512, (hf + 1) * 512)
                    pk = 3 + tb * 2 + hf
                    tt("dve", t2[:, tb, hs], PS[pk][:, :], gt2[:, hs], ALU.mult, [psk(pk), "gt2"], [("t2", tb)])
                sch.add("dve", (lambda xt=xt, tb=tb: (lambda e: e.scalar_tensor_tensor(
                    t2[:, tb, :], xt, ALPHA, t2[:, tb, :], ALU.mult, ALU.add)))(),
                    [("x1t", gb, tb), ("t2", tb)], [("t2", tb)])
                layer_norm(t2[:, tb, :], t2[:, tb, :], ln2[:, 0, :], ln2[:, 1, :], ("t2", tb), ("t2", tb), "ln2", st6b, mvb)
                dma("sp", out_d[t], t2[:, tb, :], [("t2", tb)], [("out", t)])

        prep_load(0)
        for bi in range(4):
            prep_batch(0, bi)
        up(0, 0)
        for g in range(8):
            if g + 1 < 8:
                prep_load(g + 1)
            for q in range(4):
                if q + 1 < 4:
                    up(g, q + 1, extra=(g + 1 if (q == 1 and g + 1 < 8) else None))
                down(g, q)
            if g + 1 < 8:
                up(g + 1, 0)
            epilogue(g)
    sch.emit(nc)
    return nc


def prep_inputs(x, c, w_ada, b_ada, w_in, b_forget, w_dw, b_dw, gn_g, gn_b, g_attn_out,
                g_conv_out, w_out, ln1_g, ln1_b, w_ff1, w_ff2, ln2_g, ln2_b):
    f = np.float32
    x = np.asarray(x, f)
    w_in0 = np.asarray(w_in, f)[0]

    def kmaj(w):
        return np.ascontiguousarray(w.reshape(KC, 128, -1).transpose(1, 0, 2))

    shared = {}
    shared["wq"] = kmaj(w_in0[:, 0:512])
    shared["wk"] = kmaj(w_in0[:, 512:1024])
    wv = w_in0[:, 1024:1536]
    wf = w_in0[:, 1536:1544]
    wvf = np.stack([np.concatenate([wv[:, 0:256], wf[:, 0:4]], 1),
                    np.concatenate([wv[:, 256:512], wf[:, 4:8]], 1)], 1)
    shared["wvf"] = np.ascontiguousarray(wvf.reshape(KC, 128, 2, 260).transpose(1, 0, 2, 3))
    shared["wa"] = kmaj(w_in0[:, 1544:2056])
    shared["wg"] = kmaj(w_in0[:, 2056:2568])
    shared["wo"] = kmaj(np.asarray(w_out, f)[0])
    shared["w1"] = kmaj(np.asarray(w_ff1, f)[0])
    shared["w2"] = np.ascontiguousarray(np.asarray(w_ff2, f)[0].reshape(32, 128, D).transpose(1, 0, 2))
    shared["wada"] = kmaj(np.asarray(w_ada, f)[0])
    shared["bada"] = np.ascontiguousarray(np.asarray(b_ada, f).reshape(1, 6 * D))
    shared["bfg"] = np.ascontiguousarray(np.broadcast_to(np.asarray(b_forget, f).reshape(1, 8), (128, 8)))
    shared["wdw"] = np.ascontiguousarray(np.asarray(w_dw, f)[0, :, 0, :].reshape(31, 4, 128).transpose(2, 1, 0))
    cv = np.stack([np.asarray(a, f)[0].reshape(4, 128).T for a in (b_dw, gn_g, gn_b, g_conv_out)], -1)
    shared["cvec"] = np.ascontiguousarray(cv)
    shared["gao"] = np.ascontiguousarray(np.tile(np.asarray(g_attn_out, f)[0].reshape(8, 64).T, (2, 1)))
    shared["lnv"] = np.ascontiguousarray(np.stack(
        [np.broadcast_to(np.asarray(a, f).reshape(1, D), (128, D)) for a in (ln1_g, ln1_b, ln2_g, ln2_b)], 0))
    shared["ident"] = np.eye(128, dtype=f)
    shared["utri"] = np.triu(np.ones((128, 128), f))
    shared["gmat"] = np.kron(np.eye(2, dtype=f), np.full((64, 64), 1.0 / 64, f))

    in_maps = []
    for core in range(8):
        b, j = core // 4, core % 4
        xb = x[b]
        m_ = dict(shared)
        m_["xT"] = np.ascontiguousarray(xb.reshape(16, 512, KC, 128).transpose(0, 3, 2, 1))
        xqT = np.zeros((NSLOT, 128, KC, HALO + QB), f)
        xq = np.zeros((16, 128, D), f)
        halo = np.ones((128, NSLOT), f)
        sel = np.zeros((128, NSLOT, NB, 8), f)
        for m in range(NSLOT):
            i = 4 * m + j
            t0 = QB * i
            blk = xb[t0:t0 + QB]
            xqT[m, :, :, HALO:] = blk.reshape(QB, KC, 128).transpose(2, 1, 0)
            if t0 >= HALO:
                xqT[m, :, :, :HALO] = xb[t0 - HALO:t0].reshape(HALO, KC, 128).transpose(2, 1, 0)
            else:
                halo[:, m] = 0.0
            xq[4 * m:4 * m + 4] = blk.reshape(4, 128, D)
            sel[:, m, 16 * m + 4 * j + 1, :] = 1.0
        m_["xqT"] = xqT
        m_["xq"] = xq
        m_["halo_ok"] = halo
        m_["sel"] = sel
        s_idx = np.arange(128)[:, None, None]
        r_idx = np.arange(16)[None, :, None]
        q_idx = np.arange(QB)[None, None, :]
        m_["mask"] = ((128 * r_idx + s_idx) <= (QB * j + q_idx)).astype(f)
        m_["cT"] = np.ascontiguousarray(np.asarray(c, f)[b].reshape(KC, 128).T)
        in_maps.append(m_)
    return in_maps


_NC_CACHE = {}


def kernel(**inputs):
    in_maps = prep_inputs(**inputs)
    if "nc" not in _NC_CACHE:
        _NC_CACHE["nc"] = build_program()
    nc = _NC_CACHE["nc"]
    res = run_bass_kernel_spmd(nc, in_maps, core_ids=list(range(8)))
    out = np.zeros((2, S, D), np.float32)
    for core in range(8):
        b, j = core // 4, core % 4
        o = np.asarray(res.results[core]["out"]).reshape(NSLOT, QB, D)
        for m in range(NSLOT):
            i = 4 * m + j
            out[b, QB * i:QB * (i + 1)] = o[m]
    return out
```

```python
import numpy as np
import concourse.bass as bass
import concourse.mybir as mybir
from concourse.bass_utils import run_bass_kernel_spmd

F32 = mybir.dt.float32
BF16 = mybir.dt.bfloat16
AF = mybir.ActivationFunctionType
ALU = mybir.AluOpType
AX = mybir.AxisListType

D = 1024
S = 8192
KC = 8
NB = 64
NSLOT = 4
QB = 512
HALO = 32
TQ = NSLOT * QB
DFF = 4096
ALPHA = 2.0 ** 0.25
EPS = 1e-5
CLAMP = 75.0
IMMEDIATE = False
N_DMA_SEMS = 24
N_SW_SEMS = 8


class Sched:
    ENGS = ("pe", "act", "dve", "pool", "sp")

    def __init__(self):
        self.ins = []
        self.lw = {}
        self.rd = {}
        self.bar = set()
        self.since = set()

    def barrier(self):
        last = {}
        for i in self.since:
            I = self.ins[i]
            if I["dma"]:
                last[("dma", i)] = i
            else:
                last[I["eng"]] = max(last.get(I["eng"], -1), i)
        self.bar = set(last.values())
        self.since = set(self.bar)

    def add(self, eng, fn, reads=(), writes=(), dma=False):
        i = len(self.ins)
        deps = set(self.bar)
        raw = set()
        self.since.add(i)
        for k in reads:
            w = self.lw.get(k)
            if w:
                deps.update(w.values())
                raw.update(w.values())
        for k in writes:
            w = self.lw.get(k)
            if w:
                deps.update(w.values())
            r = self.rd.get(k)
            if r:
                deps.update(r.values())
        wtag = "dma" if dma else eng
        for k in writes:
            self.lw.setdefault(k, {})[wtag] = i
            self.rd[k] = {}
        tag = ("dma", i) if dma else eng
        for k in reads:
            self.rd.setdefault(k, {})[tag] = i
        self.ins.append(dict(eng=eng, fn=fn, deps=deps, raw=raw, dma=dma))
        return i

    def emit(self, nc):
        ins = self.ins
        need = []
        for i, I in enumerate(ins):
            best = {}
            dmas = set()
            for d in I["deps"]:
                P = ins[d]
                if P["dma"]:
                    dmas.add(d)
                    continue
                if P["eng"] == I["eng"] and not I["dma"]:
                    if I["eng"] == "pe":
                        continue
                    if d not in I["raw"]:
                        continue
                if best.get(P["eng"], -1) < d:
                    best[P["eng"]] = d
            need.append((best, dmas))
        flagged = set()
        for best, _ in need:
            flagged.update(best.values())
        cnt = {}
        run = {e: 0 for e in self.ENGS}
        for i, I in enumerate(ins):
            if I["dma"]:
                continue
            if i in flagged:
                run[I["eng"]] += 1
            cnt[i] = run[I["eng"]]
        dma_ids = [i for i, I in enumerate(ins) if I["dma"]]
        dma_sem = {}
        prev_dma = {}
        n_hw = N_DMA_SEMS - N_SW_SEMS
        for qname, base, width in (("sp", 0, n_hw), ("pool", n_hw, N_SW_SEMS)):
            ids = [i for i in dma_ids if ins[i]["eng"] == qname]
            for k, i in enumerate(ids):
                dma_sem[i] = (base + k % width, 16 * (k // width + 1))
                prev_dma[i] = ids[k - width] if k >= width else None
        per_eng = {e: [i for i, I in enumerate(ins) if I["eng"] == e] for e in self.ENGS}
        final_dma = {}
        for i in dma_ids:
            s, v = dma_sem[i]
            final_dma[s] = max(final_dma.get(s, 0), v)

        import contextlib
        with contextlib.ExitStack() as st:
            esem = {e: st.enter_context(nc.semaphore("sem_" + e)) for e in ("pe", "act", "dve", "pool")}
            dsem = [st.enter_context(nc.semaphore("sem_dma%d" % k)) for k in range(N_DMA_SEMS)]
            block = st.enter_context(nc.Block())

            def body(eng_name):
                def run_eng(eng):
                    waited = {}

                    def wait(key, sem, val):
                        if waited.get(key, 0) >= val:
                            return
                        eng.wait_ge(sem, val)
                        waited[key] = val

                    for i in per_eng[eng_name]:
                        I = ins[i]
                        best, dmas = need[i]
                        for pe_name, d in best.items():
                            wait(pe_name, esem[pe_name], cnt[d])
                        for d in dmas:
                            s, v = dma_sem[d]
                            wait(("d", s), dsem[s], v)
                        if I["dma"]:
                            p = prev_dma[i]
                            if p is not None:
                                s, v = dma_sem[p]
                                wait(("d", s), dsem[s], v)
                        ret = I["fn"](eng)
                        if I["dma"]:
                            s, v = dma_sem[i]
                            ret.then_inc(dsem[s], 16)
                        elif i in flagged:
                            ret.then_inc(esem[eng_name], 1)
                    if eng_name == "sp":
                        for s, v in sorted(final_dma.items()):
                            wait(("d", s), dsem[s], v)
                return run_eng

            block.tensor(body("pe"))
            block.scalar(body("act"))
            block.vector(body("dve"))
            block.gpsimd(body("pool"))
            block.sync(body("sp"))


def build_program(debug=None):
    nc = bass.Bass("TRN2", target_bir_lowering=False)
    sch = Sched()
    dbg = {}

    def din(name, shape, dt=F32):
        return nc.dram_tensor(name, list(shape), dt, kind="ExternalInput").ap()

    def dout(name, shape, dt=F32):
        return nc.dram_tensor(name, list(shape), dt, kind="ExternalOutput").ap()

    xT_d = din("xT", [16, 128, KC, 512])
    xqT_d = din("xqT", [NSLOT, 128, KC, HALO + QB])
    xq_d = din("xq", [16, 128, D])
    halo_d = din("halo_ok", [128, NSLOT])
    sel_d = din("sel", [128, NSLOT, NB, 8])
    mask_d = din("mask", [128, 16, QB])
    wk_d = din("wk", [128, KC, 512])
    wq_d = din("wq", [128, KC, 512])
    wvf_d = din("wvf", [128, KC, 2, 260])
    wa_d = din("wa", [128, KC, 512])
    wg_d = din("wg", [128, KC, 512])
    wo_d = din("wo", [128, KC, D])
    w1_d = din("w1", [128, KC, DFF])
    w2_d = din("w2", [128, 32, D])
    wada_d = din("wada", [128, KC, 6 * D])
    cT_d = din("cT", [128, KC])
    bada_d = din("bada", [1, 6 * D])
    bf_d = din("bfg", [128, 8])
    wdw_d = din("wdw", [128, 4, 31])
    cvec_d = din("cvec", [128, 4, 4])
    gao_d = din("gao", [128, 8])
    lnv_d = din("lnv", [4, 128, D])
    ident_d = din("ident", [128, 128])
    utri_d = din("utri", [128, 128])
    gmat_d = din("gmat", [128, 128])
    out_d = dout("out", [16, 128, D])

    import contextlib
    st = contextlib.ExitStack()

    def sb(name, shape, dt=F32):
        return st.enter_context(nc.sbuf_tensor("s_" + name, list(shape), dt))

    PS = [st.enter_context(nc.psum_tensor("ps%d" % k, [128, 512], F32)) for k in range(8)]

    def psk(k):
        return ("ps", k)

    def dma(eng, out, in_, reads, writes):
        if eng == "sp":
            sch.add("sp", lambda e: e.dma_start(out=out, in_=in_), reads, writes, dma=True)
        else:
            sch.add("pool", lambda e: e.dma_start(out=out, in_=in_), reads, writes, dma=True)

    def mm(out, lhsT, rhs, start, stop, reads, writes):
        sch.add("pe", lambda e: e.matmul(out, lhsT, rhs, start=start, stop=stop), reads, writes)

    def act(out, in_, func, reads, writes, bias=None, scale=None):
        kw = {}
        if bias is not None:
            kw["bias"] = bias
        if scale is not None:
            kw["scale"] = scale
        sch.add("act", lambda e: e.activation(out, in_, func, **kw), reads, writes)

    def ts(eng, out, in0, s1, s2, op0, op1, reads, writes):
        if op1 is None:
            sch.add(eng, lambda e: e.tensor_scalar(out, in0, s1, None, op0), reads, writes)
        else:
            sch.add(eng, lambda e: e.tensor_scalar(out, in0, s1, s2, op0, op1), reads, writes)

    def tt(eng, out, in0, in1, op, reads, writes):
        sch.add(eng, lambda e: e.tensor_tensor(out, in0, in1, op), reads, writes)

    def cp(eng, out, in_, reads, writes):
        if eng == "act":
            act(out, in_, AF.Copy, reads, writes)
        else:
            sch.add(eng, lambda e: e.tensor_copy(out, in_), reads, writes)

    def memset(eng, ap, val, writes):
        sch.add(eng, lambda e: e.memset(ap, val), (), writes)

    ident = sb("ident", [128, 128])
    utri = sb("utri", [128, 128])
    ones_f = sb("ones_f", [128, 128])
    dma("sp", ident[:], ident_d, (), ["ident"])
    dma("sp", utri[:], utri_d, (), ["utri"])
    memset("pool", ones_f[:], 1.0, ["ones_f"])

    cT = sb("cT", [128, KC])
    sc_c = sb("sc_c", [128, KC])
    pv = sb("pv", [128, 32])
    scp1 = sb("scp1", [128, 16])
    gt_scr = nc.dram_tensor("gt_scr", [128, 2, D], F32, kind="Internal").ap()
    bfg = sb("bfg", [128, 8])
    dma("sp", cT[:], cT_d, (), ["cT"])
    dma("sp", bfg[:], bf_d, (), ["bfg"])

    act(sc_c[:], cT[:], AF.Silu, ["cT"], ["sc_c"])
    with nc.sbuf_tensor("s_wada_buf", [128, 2, KC, 512], F32) as wab, \
            nc.sbuf_tensor("s_bada", [1, 6 * D], F32) as bada, \
            nc.sbuf_tensor("s_adarow", [1, 6 * D], F32) as adarow, \
            nc.sbuf_tensor("s_gtp1", [128, 2, D], F32) as gtp1:
        dma("sp", bada[:], bada_d, (), ["bada"])
        for q in range(12):
            b = q % 2
            dma("sp", wab[:, b], wada_d[:, :, q * 512:(q + 1) * 512], (), [("wab", b)])
            pk = q % 2
            for kc in range(KC):
                mm(PS[pk][0:1, :], sc_c[:, kc:kc + 1], wab[:, b, kc, :], kc == 0, kc == KC - 1,
                   ["sc_c", ("wab", b)], [psk(pk)])
            tt("dve", adarow[0:1, q * 512:(q + 1) * 512], PS[pk][0:1, :], bada[0:1, q * 512:(q + 1) * 512],
               ALU.add, [psk(pk), "bada"], ["adarow"])
        for vi, off in enumerate((0, D, 3 * D, 4 * D)):
            for kc in range(KC):
                col = vi * 8 + kc
                mm(PS[2][:, col:col + 1], adarow[0:1, off + kc * 128: off + (kc + 1) * 128], ones_f[0:1, 0:1],
                   True, True, ["adarow", "ones_f"], [psk(2)])
        cp("dve", pv[:], PS[2][:, 0:32], [psk(2)], ["pv"])
        ts("dve", scp1[:, 0:8], pv[:, 8:16], 1.0, None, ALU.add, None, ["pv"], ["scp1"])
        ts("dve", scp1[:, 8:16], pv[:, 24:32], 1.0, None, ALU.add, None, ["pv"], ["scp1"])
        for gi, off in enumerate((2 * D, 5 * D)):
            for hf in range(2):
                pk = 3 + hf
                mm(PS[pk][:, :], ones_f[0:1, 0:128], adarow[0:1, off + hf * 512: off + (hf + 1) * 512],
                   True, True, ["adarow", "ones_f"], [psk(pk)])
                ts("dve", gtp1[:, gi, hf * 512:(hf + 1) * 512], PS[pk][:, :], 1.0, None, ALU.add, None,
                   [psk(pk)], ["gtp1"])
        dma("sp", gt_scr, gtp1[:], ["gtp1"], ["gt_scr"])
    sch.barrier()
    sh1 = pv[:, 0:8]
    sh2 = pv[:, 16:24]

    if debug == "A":
        dbg["pv"] = dout("dbg_pv", [128, 32])
        dbg["gtp1"] = dout("dbg_gtp1", [128, 2, D])
        dma("sp", dbg["pv"], pv[:], ["pv", "scp1"], ["o1"])
        dma("sp", dbg["gtp1"], gt_scr, ["gt_scr"], ["o2"])
        sch.emit(nc)
        st.close()
        return nc

    R1 = sb("R1", [128, 6144])

    def view(ap, dt, pat=None, **kw):
        v = ap if dt == F32 else ap.bitcast(dt)
        return v.rearrange(pat, **kw) if pat else v

    ATn = view(R1[:, 0:4096], BF16, "p (a b) -> p a b", a=4)
    accA = R1[:, 4096:6144]
    stE = contextlib.ExitStack()

    def sbE(name, shape, dt=F32):
        return stE.enter_context(nc.sbuf_tensor("s_" + name, list(shape), dt))

    KT = sbE("KT", [128, 4, S], BF16)
    Vtf = sbE("Vt", [128, NB * 520 + 64], BF16)
    Vt = Vtf[:, 0:NB * 520].rearrange("p (b h d) -> p b h d", b=NB, h=8)
    fl = sbE("fl", [128, NB, 8])
    memset("pool", Vt[:, :, :, 64:65], 1.0, ["Vones"])
    memset("pool", Vtf[:, NB * 520:NB * 520 + 64], 0.0, ["Vones"])

    with contextlib.ExitStack() as stB:
        def sbB(name, shape, dt=F32):
            return stB.enter_context(nc.sbuf_tensor("s_" + name, list(shape), dt))
        wk = sbB("wk", [128, KC, 512], BF16)
        wvf = sbB("wvf", [128, KC, 2, 260], BF16)
        xs = sbB("xs", [128, 1, KC, QB])
        uT = sbB("uT", [128, 2, KC, QB], BF16)
        dma("pool", wk[:], wk_d, (), ["wk"])
        dma("pool", wvf[:], wvf_d, (), ["wvf"])

        def make_u(xs, uT, buf, width, it, xbuf=None):
            for kc in range(KC):
                eng = ("pool", "dve")[(kc + it) % 2] if kc % 4 != 3 else "act"
                o = uT[:, buf, kc, 0:width]
                xb = buf if xbuf is None else xbuf
                i_ = xs[:, xb, kc, 0:width]
                rk = [("xs", xb, kc // 4), "scp1", "pv"]
                wkk = [("uT", buf, kc)]
                if eng == "act":
                    act(o, i_, AF.Identity, rk, wkk, bias=sh1[:, kc:kc + 1], scale=scp1[:, kc:kc + 1])
                else:
                    ts(eng, o, i_, scp1[:, kc:kc + 1], sh1[:, kc:kc + 1], ALU.mult, ALU.add, rk, wkk)

        it = 0
        for c in range(16):
            buf = it % 2
            dma("sp", xs[:, 0, 0:4, :], xT_d[c, :, 0:4, :], (), [("xs", 0, 0)])
            dma("sp", xs[:, 0, 4:8, :], xT_d[c, :, 4:8, :], (), [("xs", 0, 1)])
            make_u(xs, uT, buf, 512, it, xbuf=0)
            for p in range(4):
                pk = p
                for kc in range(KC):
                    mm(PS[pk][:, :], wk[:, kc, p * 128:(p + 1) * 128], uT[:, buf, kc, 0:512], kc == 0, kc == KC - 1,
                       ["wk", ("uT", buf, kc)], [psk(pk)])
                cp("act", KT[:, p, c * 512:(c + 1) * 512], PS[pk][:, :], [psk(pk)], [("KT", p, c)])
            for tb in range(4):
                blk = c * 4 + tb
                for hf in range(2):
                    pk = 4 + (tb * 2 + hf) % 4
                    for kc in range(KC):
                        mm(PS[pk][:, 0:260], uT[:, buf, kc, tb * 128:(tb + 1) * 128], wvf[:, kc, hf, :],
                           kc == 0, kc == KC - 1, ["wvf", ("uT", buf, kc)], [psk(pk)])
                    cp("dve", Vt[:, blk, hf * 4:(hf + 1) * 4, 0:64],
                       PS[pk][:, 0:256].rearrange("p (h d) -> p h d", h=4), [psk(pk)], [("Vt", c)])
                    cp("dve", fl[:, blk, hf * 4:(hf + 1) * 4], PS[pk][:, 256:260], [psk(pk)], ["fl"])
            it += 1
    sch.barrier()
    QT = sbE("QT", [128, 4, TQ], BF16)
    with contextlib.ExitStack() as stB:
        wq = sbB("wq", [128, KC, 512], BF16)
        xs = sbB("xs2", [128, 1, KC, HALO + QB])
        uT = sbB("uT2", [128, 1, KC, HALO + QB], BF16)
        dma("pool", wq[:], wq_d, (), ["wq"])
        for m in range(NSLOT):
            buf = 0
            dma("sp", xs[:, buf], xqT_d[m], (), [("xs", buf, 0), ("xs", buf, 1)])
            make_u(xs, uT, buf, HALO + QB, it)
            for p in range(4):
                pk = p
                for kc in range(KC):
                    mm(PS[pk][:, :], wq[:, kc, p * 128:(p + 1) * 128], uT[:, buf, kc, HALO:HALO + QB],
                       kc == 0, kc == KC - 1, ["wq", ("uT", buf, kc)], [psk(pk)])
                cp("act", QT[:, p, m * QB:(m + 1) * QB], PS[pk][:, :], [psk(pk)], [("QT", p, m)])
            it += 1

    sch.barrier()
    if debug == "B":
        dbg["KT"] = dout("dbg_KT", [128, 4, S], BF16)
        dbg["Vt"] = dout("dbg_Vt", [128, NB, 8, 65], BF16)
        dbg["QT"] = dout("dbg_QT", [128, 4, TQ], BF16)
        dbg["fl"] = dout("dbg_fl", [128, NB, 8])
        dma("sp", dbg["KT"], KT[:], [("KT", p, c) for p in range(4) for c in range(16)], ["o1"])
        dma("sp", dbg["Vt"], Vt, [("Vt", c) for c in range(16)] + ["Vones"], ["o2"])
        dma("sp", dbg["QT"], QT[:], [("QT", p, m) for p in range(4) for m in range(4)], ["o3"])
        dma("sp", dbg["fl"], fl[:], ["fl"], ["o4"])
        sch.emit(nc)
        return nc

    bias = sbE("bias", [128, 8, NSLOT, NB])
    with contextlib.ExitStack() as stB:
        nbf = sbB("nbf", [128, 8])
        ee = sbB("ee", [128, NB, 8])
        lg = sbB("lg", [128, NB, 8])
        Ta = sbB("Ta", [128, NB, 8])
        Tb = sbB("Tb", [128, NB, 8])
        Tc = sbB("Tc", [128, NB, 8])
        G_tm = sbB("G_tm", [128, NB, 8])
        sel4 = sbB("sel4", [128, NSLOT, NB, 8])
        prod = sbB("prod", [128, NB, 8])
        Gref = sbB("Gref", [128, NSLOT, 8])
        dma("sp", sel4[:], sel_d, (), ["sel4"])
        if debug == "D0":
            dbg["G"] = dout("dbg_G", [128, NB, 8])
            dma("sp", dbg["G"], fl[:], ["fl"], ["o1"])
            sch.emit(nc)
            return nc
        ts("dve", nbf[:], bfg[:], -1.0, None, ALU.mult, None, ["bfg"], ["nbf"])
        for h in range(8):
            act(ee[:, :, h], fl[:, :, h], AF.Exp, ["fl", "nbf"], ["ee"], bias=nbf[:, h:h + 1], scale=-1.0)
        act(lg[:].rearrange("p b h -> p (b h)"), ee[:].rearrange("p b h -> p (b h)"), AF.Ln, ["ee"], ["lg"],
            bias=1.0, scale=1.0)

        if debug == "D1":
            dbg["G"] = dout("dbg_G", [128, NB, 8])
            dma("sp", dbg["G"], lg[:], ["lg"], ["o1"])
            sch.emit(nc)
            return nc
        lg2 = lg[:].rearrange("p b h -> p (b h)")
        mm(PS[0][:, :], utri[:], lg2, True, True, ["utri", "lg"], [psk(0)])
        mm(PS[1][:, :], ones_f[:], lg2, True, True, ["ones_f", "lg"], [psk(1)])
        cp("dve", Ta[:].rearrange("p b h -> p (b h)"), PS[1][:, :], [psk(1)], ["Ta"])
        cp("dve", Tc[:].rearrange("p b h -> p (b h)"), PS[1][:, :], [psk(1)], ["Tc"])

        if debug == "D2":
            dbg["G"] = dout("dbg_G", [128, NB, 8])
            dma("sp", dbg["G"], Tc[:], ["Tc"], ["o1"])
            sch.emit(nc)
            return nc
        src, dst, sk, dk = Ta, Tb, "Ta", "Tb"
        for dd in (1, 2, 4, 8, 16, 32):
            tt("dve", dst[:, dd:, :], src[:, dd:, :], src[:, :NB - dd, :], ALU.add, [sk], [dk])
            cp("dve", dst[:, :dd, :], src[:, :dd, :], [sk], [dk])
            src, dst, sk, dk = dst, src, dk, sk
        Pinc, pk_ = src, sk

        if debug == "D3":
            dbg["G"] = dout("dbg_G", [128, NB, 8])
            dma("sp", dbg["G"], Pinc[:], [pk_], ["o1"])
            sch.emit(nc)
            return nc
        tt("dve", Tc[:], Pinc[:], Tc[:], ALU.subtract, [pk_, "Tc"], ["Tc"])
        tt("dve", G_tm[:].rearrange("p b h -> p (b h)"), PS[0][:, :], Tc[:].rearrange("p b h -> p (b h)"),
           ALU.add, [psk(0), "Tc"], ["G_tm"])

        if debug == "D4":
            dbg["G"] = dout("dbg_G", [128, NB, 8])
            dma("sp", dbg["G"], G_tm[:], ["G_tm"], ["o1"])
            sch.emit(nc)
            return nc
        for m in range(NSLOT):
            tt("dve", prod[:], Pinc[:], sel4[:, m], ALU.mult, [pk_, "sel4"], ["prod"])
            sch.add("dve", (lambda m=m: (lambda e: e.tensor_reduce(
                Gref[:, m, :], prod[:].rearrange("p b h -> p h b"), AX.X, ALU.add)))(),
                ["prod"], ["Gref"])
        if debug == "D5":
            dbg["G"] = dout("dbg_G", [128, NSLOT, 8])
            dma("sp", dbg["G"], Gref[:], ["Gref"], ["o1"])
            sch.emit(nc)
            return nc
        for h in range(8):
            for m in range(NSLOT):
                ts("dve", bias[:, h, m, :], G_tm[:, :, h], Gref[:, m, h:h + 1], CLAMP, ALU.subtract, ALU.min,
                   ["G_tm", "Gref"], ["bias"])
        if debug == "D":
            dbg["G"] = dout("dbg_G", [128, NB, 8])
            dbg["bias"] = dout("dbg_bias", [128, 8, NSLOT, NB])
            dma("sp", dbg["G"], G_tm[:], ["G_tm"], ["o1"])
            dma("sp", dbg["bias"], bias[:], ["bias"], ["o2"])
            sch.emit(nc)
            return nc
    sch.barrier()

    maskt = sbE("maskt", [128, 16, QB], BF16)
    gao = sbE("gao", [128, 8])
    Pt = sbE("Pt", [128, 3, QB], BF16)
    Ot = sbE("Ot", [128, 1, QB])
    af = sbE("af", [128, QB])
    sqs = sbE("sqs", [128, QB])
    rl = sqs
    atsc = sbE("atsc", [128, QB], BF16)
    dma("pool", maskt[:], mask_d, (), ["mask"])
    dma("sp", gao[:], gao_d, (), ["gao"])
    ug = 0
    pm = 0
    pending = []

    def make_epilogue(p, m, obank):
        qs = slice(m * QB, (m + 1) * QB)
        stages = []
        for e in (0, 1):
            h = 2 * p + e
            ob = obank[e]

            def s1(ob=ob):
                cp("dve", Ot[0:65, 0, :], PS[ob][0:65, :], [psk(ob)], ["Ot"])

            def s2():
                sch.add("dve", lambda en: en.reciprocal(rl[64:65, :], Ot[64:65, 0, :]), ["Ot"], ["sqs"])

            def s3(ob=ob):
                mm(PS[ob][0:64, :], ones_f[64:65, 0:64], rl[64:65, :], True, True, ["sqs", "ones_f", "Ot"], [psk(ob)])

            def s4(ob=ob):
                tt("dve", af[0:64, :], Ot[0:64, 0, :], PS[ob][0:64, :], ALU.mult, ["Ot", psk(ob)], ["af"])

            def s5(e=e, h=h):
                if e == 0:
                    ts("dve", ATn[0:64, p, qs], af[0:64, :], gao[0:64, h:h + 1], None, ALU.mult, None,
                       ["af", "gao"], [("ATn", p, m, 0)])
                else:
                    ts("dve", atsc[0:64, :], af[0:64, :], gao[0:64, h:h + 1], None, ALU.mult, None,
                       ["af", "gao"], ["atsc"])
                    dma("sp", ATn[64:128, p, qs], atsc[0:64, :], ["atsc"], [("ATn", p, m, 1)])

            def s6(h=h):
                if h == 0:
                    tt("pool", accA[0:64, qs], af[0:64, :], af[0:64, :], ALU.mult, ["af"], [("accA", m)])
                else:
                    tt("pool", sqs[0:64, :], af[0:64, :], af[0:64, :], ALU.mult, ["af"], ["sqs"])
                    tt("pool", accA[0:64, qs], accA[0:64, qs], sqs[0:64, :], ALU.add, ["sqs", ("accA", m)],
                       [("accA", m)])
            stages += [s1, s2, s3, s4, s5, s6]
        return stages

    for p in range(4):
        for m in range(NSLOT):
            nkb = 16 * m + 16
            obank = [4 + 2 * (pm % 2), 5 + 2 * (pm % 2)]
            qs = slice(m * QB, (m + 1) * QB)

            def qkpair(kb):
                for e in (0, 1):
                    sbk = 2 * (kb % 2) + e
                    mm(PS[sbk][:, :], KT[e * 64:(e + 1) * 64, p, kb * 128:(kb + 1) * 128],
                       QT[e * 64:(e + 1) * 64, p, qs], True, True,
                       [("KT", p, kb // 4), ("QT", p, m)], [psk(sbk)])

            qkpair(0)
            for kb in range(nkb):
                if kb + 1 < nkb:
                    qkpair(kb + 1)
                for e in (0, 1):
                    h = 2 * p + e
                    sbk = 2 * (kb % 2) + e
                    pi = (ug + 2 * kb + e) % 3
                    act(Pt[:, pi, :], PS[sbk][:, :], AF.Exp, [psk(sbk), "bias"], [("Pt", pi)],
                        bias=bias[:, h, m, kb:kb + 1], scale=0.125)
                    if kb >= 16 * m:
                        tt("dve", Pt[:, pi, :], Pt[:, pi, :], maskt[:, kb - 16 * m, :], ALU.mult,
                           [("Pt", pi), "mask"], [("Pt", pi)])
                    v0 = kb * 520 + h * 65
                    mm(PS[obank[e]][:, :], Vtf[:, v0:v0 + 128], Pt[:, pi, :], kb == 0, kb == nkb - 1,
                       [("Vt", kb // 4), "Vones", ("Pt", pi)], [psk(obank[e])])
                if pending:
                    pending.pop(0)()
            while pending:
                pending.pop(0)()
            ug += 2 * nkb
            pending = make_epilogue(p, m, obank)
            if IMMEDIATE:
                while pending:
                    pending.pop(0)()
            pm += 1
    while pending:
        pending.pop(0)()
    if debug == "E":
        dbg["ATn"] = dout("dbg_ATn", [128, 4, TQ], BF16)
        dbg["accA"] = dout("dbg_accA", [128, TQ])
        dma("sp", dbg["ATn"], ATn, [("ATn", p, m, e) for p in range(4) for m in range(4) for e in (0, 1)], ["o1"])
        dma("sp", dbg["accA"], accA, [("accA", m) for m in range(4)], ["o2"])
        sch.emit(nc)
        return nc
    stE.close()
    sch.barrier()


    x1_scr = nc.dram_tensor("x1_scr", [16, 128, D], F32, kind="Internal").ap()
    R2 = sb("R2", [128, 6144])
    CT = view(R2[:, 0:4096], BF16, "p (a b) -> p a b", a=4)
    accC = R2[:, 4096:6144]
    with contextlib.ExitStack() as stB:
        wa = sbB("wa", [128, KC, 512], BF16)
        wg = sbB("wg", [128, KC, 512], BF16)
        xs = sbB("xs3", [128, 1, KC, HALO + QB])
        uT = sbB("uT3", [128, 2, KC, HALO + QB], BF16)
        ucv = sbB("ucv", [128, 2, 4, HALO + QB], BF16)
        diag = sbB("diag", [128, 4, 31, 128], BF16)
        wdw = sbB("wdw", [128, 4, 31])
        cvec = sbB("cvec", [128, 4, 4])
        halo = sbB("halo", [128, NSLOT])
        gmat = sbB("gmat", [128, 128])
        sig = sbB("sig", [128, 2, HALO + QB])
        ysb = sbB("ysb", [128, 4, QB])
        dsb = sbB("dsb", [128, 4, QB])
        sq4 = sbB("sq4", [128, 2, QB])
        lnv_ = sbB("lnv_", [128, 2, QB])
        rstd = sbB("rstd", [128, 4, QB])
        co = sbB("co", [128, 2, QB])
        csq = sbB("csq", [128, QB])
        dma("pool", wa[:], wa_d, (), ["wa"])
        dma("pool", wg[:], wg_d, (), ["wg"])
        dma("sp", wdw[:], wdw_d, (), ["wdw"])
        dma("sp", cvec[:], cvec_d, (), ["cvec"])
        dma("sp", halo[:], halo_d, (), ["halo"])
        dma("sp", gmat[:], gmat_d, (), ["gmat"])
        W = HALO + QB

        def f1a(m):
            dma("sp", xs[:, 0], xqT_d[m], (), [("xs", 0, 0), ("xs", 0, 1)])
            make_u(xs, uT, m % 2, W, m, xbuf=0)

        def f1(m):
            ub = m % 2
            for cc in range(4):
                pa, pg = cc % 2, 2 + cc % 2
                for (ps_, w_, nm) in ((pa, wa, "wa"), (pg, wg, "wg")):
                    for kc in range(KC):
                        mm(PS[ps_][:, :], w_[:, kc, cc * 128:(cc + 1) * 128], uT[:, ub, kc, HALO:W],
                           kc == 0, kc == KC - 1, [nm, ("uT", ub, kc)], [psk(ps_)])
                    for kc in range(KC):
                        mm(PS[4 + ps_][:, 0:HALO], w_[:, kc, cc * 128:(cc + 1) * 128], uT[:, ub, kc, 0:HALO],
                           kc == 0, kc == KC - 1, [nm, ("uT", ub, kc)], [psk(4 + ps_)])
                sb_ = cc % 2
                act(sig[:, sb_, HALO:W], PS[pg][:, :], AF.Sigmoid, [psk(pg)], [("sig", sb_)])
                act(sig[:, sb_, 0:HALO], PS[4 + pg][:, 0:HALO], AF.Sigmoid, [psk(4 + pg)], [("sig", sb_)])
                tt("dve", ucv[:, ub, cc, HALO:W], PS[pa][:, :], sig[:, sb_, HALO:W], ALU.mult,
                   [psk(pa), ("sig", sb_)], [("ucv", ub, cc)])
                sch.add("dve", (lambda cc=cc, pa=pa, sb_=sb_, m=m, ub=ub: (lambda e: e.scalar_tensor_tensor(
                    ucv[:, ub, cc, 0:HALO], PS[4 + pa][:, 0:HALO], halo[:, m:m + 1], sig[:, sb_, 0:HALO],
                    ALU.mult, ALU.mult)))(), [psk(4 + pa), ("sig", sb_), "halo"], [("ucv", ub, cc)])
                drain_diag(16)

        def f2(m):
            ub = m % 2
            qs = slice(m * QB, (m + 1) * QB)
            for cc in range(4):
                pk = cc % 2
                for k in range(31):
                    mm(PS[pk][:, :], diag[:, cc, k, :], ucv[:, ub, cc, 2 + k: 2 + k + QB], k == 0, k == 30,
                       [("diag", cc), ("ucv", ub, cc)], [psk(pk)])
                ts("dve", ysb[:, cc, :], PS[pk][:, :], cvec[:, cc, 0:1], None, ALU.add, None,
                   [psk(pk), "cvec"], [("ysb", cc)])
            for cc in range(4):
                pm_, pv_ = 2 + cc % 2, 4 + cc % 2
                mm(PS[pm_][:, :], gmat[:], ysb[:, cc, :], True, True, ["gmat", ("ysb", cc)], [psk(pm_)])
                tt("dve", dsb[:, cc, :], ysb[:, cc, :], PS[pm_][:, :], ALU.subtract, [("ysb", cc), psk(pm_)],
                   [("dsb", cc)])
                tt("dve", sq4[:, cc % 2, :], dsb[:, cc, :], dsb[:, cc, :], ALU.mult, [("dsb", cc)], [("sq4", cc % 2)])
                mm(PS[pv_][:, :], gmat[:], sq4[:, cc % 2, :], True, True, ["gmat", ("sq4", cc % 2)], [psk(pv_)])
                act(lnv_[:, cc % 2, :], PS[pv_][:, :], AF.Ln, [psk(pv_)], [("lnv_", cc % 2)], bias=EPS, scale=1.0)
                act(rstd[:, cc, :], lnv_[:, cc % 2, :], AF.Exp, [("lnv_", cc % 2)], [("rstd", cc)], scale=-0.5)
                tt("dve", dsb[:, cc, :], dsb[:, cc, :], rstd[:, cc, :], ALU.mult, [("dsb", cc), ("rstd", cc)],
                   [("dsb", cc)])
            for cc in range(4):
                cb = cc % 2
                act(co[:, cb, :], dsb[:, cc, :], AF.Silu, [("dsb", cc), "cvec"], [("co", cb)],
                    bias=cvec[:, cc, 2:3], scale=cvec[:, cc, 1:2])
                ts("dve", CT[:, cc, qs], co[:, cb, :], cvec[:, cc, 3:4], None, ALU.mult, None,
                   [("co", cb), "cvec"], [("CT", cc, m)])
                if cc == 0:
                    tt("dve", accC[:, qs], co[:, cb, :], co[:, cb, :], ALU.mult, [("co", cb)], [("accC", m)])
                else:
                    tt("dve", csq[:], co[:, cb, :], co[:, cb, :], ALU.mult, [("co", cb)], ["csq"])
                    tt("dve", accC[:, qs], accC[:, qs], csq[:], ALU.add, ["csq", ("accC", m)], [("accC", m)])

        diag_ops = []

        def mk_diag(cc, k):
            def go():
                if k % 2 == 0:
                    ts("dve", diag[:, cc, k, :], ident[:], wdw[:, cc, k:k + 1], None, ALU.mult, None,
                       ["ident", "wdw"], [("diag", cc)])
                else:
                    act(diag[:, cc, k, :], ident[:], AF.Identity, ["ident", "wdw"], [("diag", cc)],
                        scale=wdw[:, cc, k:k + 1], bias=0.0)
            return go

        for cc in range(4):
            for k in range(31):
                diag_ops.append(mk_diag(cc, k))

        def drain_diag(n):
            for _ in range(n):
                if diag_ops:
                    diag_ops.pop(0)()

        f1a(0)
        f1a(1)
        f1(0)
        for m in range(NSLOT):
            if m + 1 < NSLOT:
                f1(m + 1)
                if m + 2 < NSLOT:
                    f1a(m + 2)
            drain_diag(1000)
            f2(m)
        if debug == "F":
            dbg["CT"] = dout("dbg_CT", [128, 4, TQ], BF16)
            dbg["accC"] = dout("dbg_accC", [128, TQ])
            dma("sp", dbg["CT"], CT, [("CT", cc, m) for cc in range(4) for m in range(4)], ["o1"])
            dma("sp", dbg["accC"], accC, [("accC", m) for m in range(4)], ["o2"])
            sch.emit(nc)
            return nc
    sch.barrier()

    w1 = sb("w1", [128, KC, DFF], BF16)
    rsd = sb("rsd", [128, 2, 16])
    with contextlib.ExitStack() as stB:
        wo = sbB("wo", [128, KC, D], BF16)
        ln1 = sbB("ln1", [128, 2, D])
        gt1 = sbB("gt1", [128, D])
        xqt = sbB("xqt", [128, 8, D])
        tA = sbB("tA", [128, 8, D])
        st6 = sbB("st6", [128, 2, 6])
        mv = sbB("mv", [128, 4])
        lnr = sbB("lnr", [128, 2, 16])
        dma("pool", wo[:], wo_d, (), ["wo"])
        for kc in range(KC):
            dma("pool", w1[:, kc, :], w1_d[:, kc, :], (), [("w1", kc)])
        dma("sp", ln1[:, 0, :], lnv_d[0], (), ["ln1"])
        dma("sp", ln1[:, 1, :], lnv_d[1], (), ["ln1"])
        dma("sp", gt1[:], gt_scr[:, 0, :], ["gt_scr"], ["gt1"])
        for kc in range(KC):
            tt(("dve", "pool")[kc % 2], wo[:, kc, :], wo[:, kc, :], gt1[:], ALU.mult, ["wo", "gt1"], ["wo"])
        for t in range(16):
            mm(PS[6][:, t:t + 1], accA[0:64, t * 128:(t + 1) * 128], ones_f[0:64, 0:1], True, True,
               [("accA", t // 4), "ones_f"], [psk(6)])
        for t in range(16):
            mm(PS[6][:, 16 + t:17 + t], accC[:, t * 128:(t + 1) * 128], ones_f[:, 0:1], True, True,
               [("accC", t // 4), "ones_f"], [psk(6)])
        act(lnr[:].rearrange("p a b -> p (a b)"), PS[6][:, 0:32], AF.Ln, [psk(6)], ["lnr"], bias=EPS, scale=1.0 / 512)
        act(rsd[:].rearrange("p a b -> p (a b)"), lnr[:].rearrange("p a b -> p (a b)"), AF.Exp, ["lnr"], ["rsd"],
            scale=-0.5)

        def layer_norm(src, dst, gam, bet, key_src, key_dst, gkey, st6, mv):
            for hf in range(2):
                sch.add("dve", (lambda hf=hf: (lambda e: e.bn_stats(st6[:, hf, :], src[:, hf * 512:(hf + 1) * 512])))(),
                        [key_src], ["st6"])
            sch.add("dve", lambda e: e.bn_aggr(mv[:, 0:2], st6[:].rearrange("p a b -> p (a b)")), ["st6"], ["mv"])
            act(mv[:, 2:3], mv[:, 1:2], AF.Ln, ["mv"], ["mv2"], bias=EPS, scale=1.0)
            act(mv[:, 2:3], mv[:, 2:3], AF.Exp, ["mv2"], ["mv2"], scale=-0.5)
            ts("dve", mv[:, 3:4], mv[:, 0:1], -1.0, mv[:, 2:3], ALU.mult, ALU.mult, ["mv", "mv2"], ["mv3"])
            act(dst, src, AF.Identity, [key_src, "mv2", "mv3"], [key_dst], bias=mv[:, 3:4], scale=mv[:, 2:3])
            tt("dve", dst, dst, gam, ALU.mult, [key_dst, gkey], [key_dst])
            tt("dve", dst, dst, bet, ALU.add, [key_dst, gkey], [key_dst])

        st64 = sbB("st64", [128, 8, 2, 6])
        mv4 = sbB("mv4", [128, 8, 4])

        def g_p1(t0):
            tiles = [t for t in (t0, t0 + 1) if t < 16]
            for t in tiles:
                b4 = t % 8
                pb = 4 * (t % 2)
                tsl = slice(t * 128, (t + 1) * 128)
                dma("sp", xqt[:, b4, :], xq_d[t], (), [("xqt", b4)])
                for hf in range(2):
                    for p in range(4):
                        mm(PS[pb + hf][:, :], ATn[:, p, tsl], wo[:, p, hf * 512:(hf + 1) * 512], p == 0, p == 3,
                           [("ATn", p, t // 4, 0), ("ATn", p, t // 4, 1), "wo"], [psk(pb + hf)])
                    for cc in range(4):
                        mm(PS[pb + 2 + hf][:, :], CT[:, cc, tsl], wo[:, 4 + cc, hf * 512:(hf + 1) * 512], cc == 0, cc == 3,
                           [("CT", cc, t // 4), "wo"], [psk(pb + 2 + hf)])
            for hf in range(2):
                hs = slice(hf * 512, (hf + 1) * 512)
                for t in tiles:
                    b4 = t % 8
                    pb = 4 * (t % 2)
                    act(tA[:, b4, hs], PS[pb + hf][:, :], AF.Identity, [psk(pb + hf), "rsd"], [("tA", b4)],
                        scale=rsd[:, 0, t:t + 1], bias=0.0)
                for t in tiles:
                    b4 = t % 8
                    pb = 4 * (t % 2)
                    sch.add("dve", (lambda hf=hf, hs=hs, b4=b4, t=t, pb=pb: (lambda e: e.scalar_tensor_tensor(
                        tA[:, b4, hs], PS[pb + 2 + hf][:, :], rsd[:, 1, t:t + 1], tA[:, b4, hs], ALU.mult, ALU.add)))(),
                        [psk(pb + 2 + hf), "rsd", ("tA", b4)], [("tA", b4)])
            for t in tiles:
                b4 = t % 8
                sch.add("dve", (lambda b4=b4: (lambda e: e.scalar_tensor_tensor(
                    xqt[:, b4, :], xqt[:, b4, :], ALPHA, tA[:, b4, :], ALU.mult, ALU.add)))(),
                    [("xqt", b4), ("tA", b4)], [("xqt", b4)])

        def g_lna(t):
            b4 = t % 8
            for hf in range(2):
                sch.add("dve", (lambda hf=hf, b4=b4: (lambda e: e.bn_stats(
                    st64[:, b4, hf, :], xqt[:, b4, hf * 512:(hf + 1) * 512])))(), [("xqt", b4)], [("st6", b4)])
            sch.add("dve", (lambda b4=b4: (lambda e: e.bn_aggr(
                mv4[:, b4, 0:2], st64[:, b4].rearrange("p a b -> p (a b)"))))(), [("st6", b4)], [("mv", b4)])
            act(mv4[:, b4, 2:3], mv4[:, b4, 1:2], AF.Ln, [("mv", b4)], [("mv2", b4)], bias=EPS, scale=1.0)
            act(mv4[:, b4, 2:3], mv4[:, b4, 2:3], AF.Exp, [("mv2", b4)], [("mv2", b4)], scale=-0.5)

        def g_lnb(t):
            b4 = t % 8
            ts("dve", mv4[:, b4, 3:4], mv4[:, b4, 0:1], -1.0, mv4[:, b4, 2:3], ALU.mult, ALU.mult,
               [("mv", b4), ("mv2", b4)], [("mv3", b4)])
            act(tA[:, b4, :], xqt[:, b4, :], AF.Identity, [("xqt", b4), ("mv2", b4), ("mv3", b4)], [("tA", b4)],
                bias=mv4[:, b4, 3:4], scale=mv4[:, b4, 2:3])

        def g_lnc(t):
            b4 = t % 8
            tt("pool", tA[:, b4, :], tA[:, b4, :], ln1[:, 0, :], ALU.mult, [("tA", b4), "ln1"], [("tA", b4)])
            tt("pool", tA[:, b4, :], tA[:, b4, :], ln1[:, 1, :], ALU.add, [("tA", b4), "ln1"], [("tA", b4)])
            dma("sp", x1_scr[t], tA[:, b4, :], [("tA", b4)], [("x1", t)])

        def pair(fn, t0):
            for t in (t0, t0 + 1):
                if 0 <= t < 16:
                    fn(t)

        for it_ in range(0, 16 + 6, 2):
            if 0 <= it_ - 4 < 16:
                pair(g_lnb, it_ - 4)
            if it_ < 16:
                g_p1(it_)
            if 0 <= it_ - 2 < 16:
                pair(g_lna, it_ - 2)
            if 0 <= it_ - 6 < 16:
                pair(g_lnc, it_ - 6)
        if debug == "G":
            dbg["x1"] = dout("dbg_x1", [16, 128, D])
            dma("sp", dbg["x1"], x1_scr, [("x1", t) for t in range(16)], ["o1"])
            sch.emit(nc)
            return nc
    sch.barrier()

    with contextlib.ExitStack() as stB:
        w2 = sbB("w2", [128, 32, D], BF16)
        for q in range(4):
            dma("pool", w2[:, q * 8:(q + 1) * 8, :], w2_d[:, q * 8:(q + 1) * 8, :], (), [("w2", q)])
        ln2 = sbB("ln2", [128, 2, D])
        gt2 = sbB("gt2", [128, D])
        st6b = sbB("st6b", [128, 2, 6])
        mvb = sbB("mvb", [128, 4])
        dma("sp", ln2[:, 0, :], lnv_d[2], (), ["ln2"])
        dma("sp", ln2[:, 1, :], lnv_d[3], (), ["ln2"])
        dma("sp", gt2[:], gt_scr[:, 1, :], ["gt_scr"], ["gt2"])
        x1t = R1[:, 0:4096].rearrange("p (a b) -> p a b", a=4)
        hTp = view(R1[:, 4096:6144], BF16, "p (a b c) -> p a b c", a=2, b=8)
        u2g = view(R2[:, 0:2048], BF16, "p (a b c) -> p a b c", a=2, b=8)
        trl = R2[:, 2048:2560].rearrange("p (a b) -> p a b", a=2)
        t2 = R2[:, 2560:4608].rearrange("p (a b) -> p a b", a=2)
        sh2p = pv[:, 16:24]
        ui = 0
        def prep_load(g):
            gb = g % 2
            for tb in range(2):
                t = 2 * g + tb
                dma("sp", x1t[:, gb * 2 + tb, :], x1_scr[t], [("x1", t)], [("x1t", gb, tb)])

        def prep_batch(g, bi):
            gb = g % 2
            for tb in (bi // 2,):
                for kq in (bi % 2,):
                    for k4 in range(4):
                        kc = kq * 4 + k4
                        sch.add("pe", (lambda gb=gb, tb=tb, kc=kc, k4=k4: (lambda e: e.transpose(
                            PS[7][:, k4 * 128:(k4 + 1) * 128], x1t[:, gb * 2 + tb, kc * 128:(kc + 1) * 128], ident[:])))(),
                            [("x1t", gb, tb), "ident"], [psk(7)])
                    for k4 in range(4):
                        kc = kq * 4 + k4
                        o_ = u2g[:, gb, kc, tb * 128:(tb + 1) * 128]
                        i_ = PS[7][:, k4 * 128:(k4 + 1) * 128]
                        if k4 % 2 == 0:
                            act(o_, i_, AF.Identity, [psk(7), "scp1", "pv"], [("u2g", gb)],
                                bias=sh2p[:, kc:kc + 1], scale=scp1[:, 8 + kc:9 + kc])
                        else:
                            ts("dve", o_, i_, scp1[:, 8 + kc:9 + kc], sh2p[:, kc:kc + 1], ALU.mult, ALU.add,
                               [psk(7), "scp1", "pv"], [("u2g", gb)])

        ui_box = [0]

        def up(g, q, extra=None):
            gb = g % 2
            for fcl in range(8):
                if extra is not None and fcl % 2 == 1:
                    prep_batch(extra, fcl // 2)
                ui = ui_box[0]
                fc = 8 * q + fcl
                pk = ui % 3
                for kc in range(KC):
                    mm(PS[pk][:, 0:256], w1[:, kc, fc * 128:(fc + 1) * 128], u2g[:, gb, kc, :], kc == 0, kc == KC - 1,
                       [("w1", kc), ("u2g", gb)], [psk(pk)])
                tb_ = ui % 2
                act(trl[:, tb_, :], PS[pk][:, 0:256], AF.Relu, [psk(pk)], [("trl", tb_)])
                tt("dve", hTp[:, q % 2, fcl, :], trl[:, tb_, :], trl[:, tb_, :], ALU.mult, [("trl", tb_)],
                   [("hTp", q % 2, fcl)])
                ui_box[0] += 1
                if epi:
                    epi.pop(0)()

        def down(g, q):
            for tb in range(2):
                for hf in range(2):
                    pk = 3 + tb * 2 + hf
                    for fcl in range(8):
                        fc = 8 * q + fcl
                        mm(PS[pk][:, :], hTp[:, q % 2, fcl, tb * 128:(tb + 1) * 128], w2[:, fc, hf * 512:(hf + 1) * 512],
                           q == 0 and fcl == 0, q == 3 and fcl == 7, [("hTp", q % 2, fcl), ("w2", q)], [psk(pk)])

        st6c = sbB("st6c", [128, 2, 2, 6])
        mvc = sbB("mvc", [128, 2, 4])
        epi = []

        def epilogue_stages(g):
            gb = g % 2
            per_tile = []
            for tb in range(2):
                t = 2 * g + tb
                xt = x1t[:, gb * 2 + tb, :]
                dst = t2[:, tb, :]
                kd = ("t2", tb)

                def sa(tb=tb):
                    for hf in range(2):
                        hs = slice(hf * 512, (hf + 1) * 512)
                        pk = 3 + tb * 2 + hf
                        tt("dve", t2[:, tb, hs], PS[pk][:, :], gt2[:, hs], ALU.mult, [psk(pk), "gt2"], [("t2", tb)])

                def sb_(tb=tb, xt=xt, gb=gb):
                    sch.add("dve", lambda e: e.scalar_tensor_tensor(
                        t2[:, tb, :], xt, ALPHA, t2[:, tb, :], ALU.mult, ALU.add),
                        [("x1t", gb, tb), ("t2", tb)], [("t2", tb)])

                def sc(tb=tb, dst=dst, kd=kd):
                    for hf in range(2):
                        sch.add("dve", (lambda hf=hf: (lambda e: e.bn_stats(
                            st6c[:, tb, hf, :], dst[:, hf * 512:(hf + 1) * 512])))(), [kd], [("st6c", tb)])
                    sch.add("dve", lambda e: e.bn_aggr(mvc[:, tb, 0:2], st6c[:, tb].rearrange("p a b -> p (a b)")),
                            [("st6c", tb)], [("mvc", tb)])

                def sd(tb=tb):
                    act(mvc[:, tb, 2:3], mvc[:, tb, 1:2], AF.Ln, [("mvc", tb)], [("mvc2", tb)], bias=EPS, scale=1.0)
                    act(mvc[:, tb, 2:3], mvc[:, tb, 2:3], AF.Exp, [("mvc2", tb)], [("mvc2", tb)], scale=-0.5)

                def se(tb=tb, dst=dst, kd=kd):
                    ts("dve", mvc[:, tb, 3:4], mvc[:, tb, 0:1], -1.0, mvc[:, tb, 2:3], ALU.mult, ALU.mult,
                       [("mvc", tb), ("mvc2", tb)], [("mvc3", tb)])
                    act(dst, dst, AF.Identity, [kd, ("mvc2", tb), ("mvc3", tb)], [kd],
                        bias=mvc[:, tb, 3:4], scale=mvc[:, tb, 2:3])

                def sf(dst=dst, kd=kd):
                    tt("dve", dst, dst, ln2[:, 0, :], ALU.mult, [kd, "ln2"], [kd])

                def sg(dst=dst, kd=kd, t=t):
                    tt("dve", dst, dst, ln2[:, 1, :], ALU.add, [kd, "ln2"], [kd])
                    dma("sp", out_d[t], dst, [kd], [("out", t)])
                per_tile.append([sa, sb_, sc, sd, se, sf, sg])
            out = []
            for i in range(7):
                out.append(per_tile[0][i])
                out.append(per_tile[1][i])
            return out

        prep_load(0)
        for bi in range(4):
            prep_batch(0, bi)
        up(0, 0)
        for g in range(8):
            if g + 1 < 8:
                prep_load(g + 1)
            for q in range(4):
                if q + 1 < 4:
                    up(g, q + 1, extra=(g + 1 if (q == 1 and g + 1 < 8) else None))
                down(g, q)
            while epi:
                epi.pop(0)()
            epi.extend(epilogue_stages(g))
            if g + 1 < 8:
                up(g + 1, 0)
        while epi:
            epi.pop(0)()
    sch.emit(nc)
    return nc


def prep_inputs(x, c, w_ada, b_ada, w_in, b_forget, w_dw, b_dw, gn_g, gn_b, g_attn_out,
                g_conv_out, w_out, ln1_g, ln1_b, w_ff1, w_ff2, ln2_g, ln2_b):
    f = np.float32
    x = np.asarray(x, f)
    w_in0 = np.asarray(w_in, f)[0]

    def kmaj(w):
        return np.ascontiguousarray(w.reshape(KC, 128, -1).transpose(1, 0, 2))

    shared = {}
    shared["wq"] = kmaj(w_in0[:, 0:512])
    shared["wk"] = kmaj(w_in0[:, 512:1024])
    wv = w_in0[:, 1024:1536]
    wf = w_in0[:, 1536:1544]
    wvf = np.stack([np.concatenate([wv[:, 0:256], wf[:, 0:4]], 1),
                    np.concatenate([wv[:, 256:512], wf[:, 4:8]], 1)], 1)
    shared["wvf"] = np.ascontiguousarray(wvf.reshape(KC, 128, 2, 260).transpose(1, 0, 2, 3))
    shared["wa"] = kmaj(w_in0[:, 1544:2056])
    shared["wg"] = kmaj(w_in0[:, 2056:2568])
    shared["wo"] = kmaj(np.asarray(w_out, f)[0])
    shared["w1"] = kmaj(np.asarray(w_ff1, f)[0])
    shared["w2"] = np.ascontiguousarray(np.asarray(w_ff2, f)[0].reshape(32, 128, D).transpose(1, 0, 2))
    shared["wada"] = kmaj(np.asarray(w_ada, f)[0])
    shared["bada"] = np.ascontiguousarray(np.asarray(b_ada, f).reshape(1, 6 * D))
    shared["bfg"] = np.ascontiguousarray(np.broadcast_to(np.asarray(b_forget, f).reshape(1, 8), (128, 8)))
    shared["wdw"] = np.ascontiguousarray(np.asarray(w_dw, f)[0, :, 0, :].reshape(31, 4, 128).transpose(2, 1, 0))
    cv = np.stack([np.asarray(a, f)[0].reshape(4, 128).T for a in (b_dw, gn_g, gn_b, g_conv_out)], -1)
    shared["cvec"] = np.ascontiguousarray(cv)
    shared["gao"] = np.ascontiguousarray(np.tile(np.asarray(g_attn_out, f)[0].reshape(8, 64).T, (2, 1)))
    shared["lnv"] = np.ascontiguousarray(np.stack(
        [np.broadcast_to(np.asarray(a, f).reshape(1, D), (128, D)) for a in (ln1_g, ln1_b, ln2_g, ln2_b)], 0))
    shared["ident"] = np.eye(128, dtype=f)
    shared["utri"] = np.triu(np.ones((128, 128), f))
    shared["gmat"] = np.kron(np.eye(2, dtype=f), np.full((64, 64), 1.0 / 64, f))

    in_maps = []
    for core in range(8):
        b, j = core // 4, core % 4
        xb = x[b]
        m_ = dict(shared)
        m_["xT"] = np.ascontiguousarray(xb.reshape(16, 512, KC, 128).transpose(0, 3, 2, 1))
        xqT = np.zeros((NSLOT, 128, KC, HALO + QB), f)
        xq = np.zeros((16, 128, D), f)
        halo = np.ones((128, NSLOT), f)
        sel = np.zeros((128, NSLOT, NB, 8), f)
        for m in range(NSLOT):
            i = 4 * m + j
            t0 = QB * i
            blk = xb[t0:t0 + QB]
            xqT[m, :, :, HALO:] = blk.reshape(QB, KC, 128).transpose(2, 1, 0)
            if t0 >= HALO:
                xqT[m, :, :, :HALO] = xb[t0 - HALO:t0].reshape(HALO, KC, 128).transpose(2, 1, 0)
            else:
                halo[:, m] = 0.0
            xq[4 * m:4 * m + 4] = blk.reshape(4, 128, D)
            sel[:, m, 16 * m + 4 * j + 1, :] = 1.0
        m_["xqT"] = xqT
        m_["xq"] = xq
        m_["halo_ok"] = halo
        m_["sel"] = sel
        s_idx = np.arange(128)[:, None, None]
        r_idx = np.arange(16)[None, :, None]
        q_idx = np.arange(QB)[None, None, :]
        m_["mask"] = ((128 * r_idx + s_idx) <= (QB * j + q_idx)).astype(f)
        m_["cT"] = np.ascontiguousarray(np.asarray(c, f)[b].reshape(KC, 128).T)
        in_maps.append(m_)
    return in_maps


_NC_CACHE = {}


def kernel(**inputs):
    in_maps = prep_inputs(**inputs)
    if "nc" not in _NC_CACHE:
        _NC_CACHE["nc"] = build_program()
    nc = _NC_CACHE["nc"]
    res = run_bass_kernel_spmd(nc, in_maps, core_ids=list(range(8)))
    out = np.zeros((2, S, D), np.float32)
    for core in range(8):
        b, j = core // 4, core % 4
        o = np.asarray(res.results[core]["out"]).reshape(NSLOT, QB, D)
        for m in range(NSLOT):
            i = 4 * m + j
            out[b, QB * i:QB * (i + 1)] = o[m]
    return out
```
